# Optimizing a Trainium2 kernel written in Bass

```python
import math
import jax, jax.numpy as jnp
from jax import lax
import numpy as np

D_MODEL = 1024
BATCH = 8
SEQ = 4096
DEPTH = 1

MEM_LEN = 256
ATT_WIDTH = D_MODEL // 2
HEAD_DIM = 64
N_ATT_HEADS = ATT_WIDTH // HEAD_DIM
CONV_WIDTH = D_MODEL - ATT_WIDTH
CONV_K = 3
DILATED_PATTERNS = ((128, 1), (512, 4), (2048, 16))
N_MEM_HEADS = 4
MEM_HEAD_DIM = D_MODEL // N_MEM_HEADS
D_FF = 4 * D_MODEL
NORM_EPS = 1e-6
NEG_INF = -1e30
IN_COLS = 3 * ATT_WIDTH + 3 * CONV_WIDTH

kernel_name = "hybrid_dilated_attn_shortconv_block"


def rms_norm(x, g):
    xf = x.astype(jnp.float32)
    y = xf * lax.rsqrt(jnp.mean(xf * xf, axis=-1, keepdims=True) + NORM_EPS)
    return (y * g.astype(jnp.float32)).astype(x.dtype)


def dilated_window_attention(q, k, v, window, dilation):
    b, s, h, e = q.shape
    steps = window // dilation
    span = steps * dilation
    s_pad = -(-s // span) * span
    nb = s_pad // span
    pad = ((0, 0), (0, s_pad - s), (0, 0), (0, 0))

    def blocks(t):
        return jnp.pad(t, pad).reshape(b, nb, steps, dilation, h, e)

    def with_prev(t):
        prev = jnp.pad(t, ((0, 0), (1, 0), (0, 0), (0, 0), (0, 0), (0, 0)))[:, :-1]
        return jnp.concatenate([prev, t], axis=2)

    qb = blocks(q)
    kk = with_prev(blocks(k))
    vv = with_prev(blocks(v))
    scale = 1.0 / math.sqrt(e)
    scores = jnp.einsum('bnqrhe,bnkrhe->bnrhqk', qb, kk).astype(jnp.float32) * scale
    i = jnp.arange(steps)[:, None]
    j = jnp.arange(2 * steps)[None, :]
    band = (j >= i) & (j <= i + steps)
    has_prev = (jnp.arange(nb) > 0)[:, None, None]
    valid = band[None] & (has_prev | (j[None] >= steps))
    scores = jnp.where(valid[None, :, None, None], scores, NEG_INF)
    lse = jax.nn.logsumexp(scores, axis=-1)
    p = jnp.exp(scores - lse[..., None]).astype(v.dtype)
    o = jnp.einsum('bnrhqk,bnkrhe->bnqrhe', p, vv)
    o = o.reshape(b, s_pad, h, e)[:, :s]
    lse = jnp.transpose(lse, (0, 1, 4, 2, 3)).reshape(b, s_pad, h)[:, :s]
    return o, lse


def short_gated_conv(bg, cg, xc, conv_w):
    u = cg * xc
    up = jnp.pad(u, ((0, 0), (CONV_K - 1, 0), (0, 0)))
    s = u.shape[1]
    conv = sum(up[:, tap:tap + s] * conv_w[tap] for tap in range(CONV_K))
    return bg * conv


def hybrid_mixer(h, w_in, conv_w, g_attn_out, g_conv_out, w_out):
    b, s, _ = h.shape
    proj = h @ w_in
    q, k, v, bg, cg, xc = jnp.split(proj, 6, axis=-1)
    q = q.reshape(b, s, N_ATT_HEADS, HEAD_DIM)
    k = k.reshape(b, s, N_ATT_HEADS, HEAD_DIM)
    v = v.reshape(b, s, N_ATT_HEADS, HEAD_DIM)
    outs, lses = [], []
    for window, dilation in DILATED_PATTERNS:
        o, lse = dilated_window_attention(q, k, v, window, dilation)
        outs.append(o)
        lses.append(lse)
    mix_w = jax.nn.softmax(jnp.stack(lses, axis=0), axis=0)
    attn = jnp.einsum('pbsh,pbshe->bshe', mix_w, jnp.stack(outs, axis=0).astype(jnp.float32))
    attn = attn.astype(h.dtype).reshape(b, s, ATT_WIDTH)
    conv = short_gated_conv(bg, cg, xc, conv_w)
    merged = jnp.concatenate([rms_norm(attn, g_attn_out), rms_norm(conv, g_conv_out)], axis=-1)
    return merged @ w_out


def memory_cross_attention(h, mem_n, w_q_mem, w_kv_mem, w_o_mem):
    b, s, _ = h.shape
    q = (h @ w_q_mem).reshape(b, s, N_MEM_HEADS, MEM_HEAD_DIM)
    kv = mem_n @ w_kv_mem
    k, v = jnp.split(kv, 2, axis=-1)
    k = k.reshape(b, MEM_LEN, N_MEM_HEADS, MEM_HEAD_DIM)
    v = v.reshape(b, MEM_LEN, N_MEM_HEADS, MEM_HEAD_DIM)
    scores = jnp.einsum('bshe,bmhe->bhsm', q, k).astype(jnp.float32) / math.sqrt(MEM_HEAD_DIM)
    p = jax.nn.softmax(scores, axis=-1).astype(v.dtype)
    o = jnp.einsum('bhsm,bmhe->bshe', p, v).reshape(b, s, D_MODEL)
    return o @ w_o_mem


def squared_relu_mlp(h, w_up, w_down):
    a = jax.nn.relu(h @ w_up)
    return (a * a) @ w_down


def setup_inputs(seed: int = 0) -> dict:
    key = jax.random.key(seed)
    ks = jax.random.split(key, 17)
    f32 = jnp.float32

    def dense(k, fan_in, shape, gain=1.0):
        return jax.random.normal(k, shape, f32) * (gain * fan_in ** -0.5)

    def gain(k, n):
        return 1.0 + 0.02 * jax.random.normal(k, (n,), f32)

    return {
        "x": jax.random.normal(ks[0], (BATCH, SEQ, D_MODEL), f32),
        "mem": jax.random.normal(ks[1], (BATCH, MEM_LEN, D_MODEL), f32),
        "g_mix": gain(ks[2], D_MODEL),
        "w_in": dense(ks[3], D_MODEL, (D_MODEL, IN_COLS)),
        "conv_w": dense(ks[4], CONV_K, (CONV_K, CONV_WIDTH)),
        "g_attn_out": gain(ks[5], ATT_WIDTH),
        "g_conv_out": gain(ks[6], CONV_WIDTH),
        "w_out": dense(ks[7], D_MODEL, (D_MODEL, D_MODEL), 0.5),
        "g_xattn": gain(ks[8], D_MODEL),
        "g_mem": gain(ks[9], D_MODEL),
        "w_q_mem": dense(ks[10], D_MODEL, (D_MODEL, D_MODEL)),
        "w_kv_mem": dense(ks[11], D_MODEL, (D_MODEL, 2 * D_MODEL)),
        "w_o_mem": dense(ks[12], D_MODEL, (D_MODEL, D_MODEL), 0.5),
        "g_mlp": gain(ks[13], D_MODEL),
        "w_up": dense(ks[14], D_MODEL, (D_MODEL, D_FF)),
        "w_down": dense(ks[15], D_FF, (D_FF, D_MODEL), 0.5),
        "g_final": gain(ks[16], D_MODEL),
    }


def reference(x, mem, g_mix, w_in, conv_w, g_attn_out, g_conv_out, w_out,
              g_xattn, g_mem, w_q_mem, w_kv_mem, w_o_mem,
              g_mlp, w_up, w_down, g_final):
    for _ in range(DEPTH):
        x = x + hybrid_mixer(rms_norm(x, g_mix), w_in, conv_w, g_attn_out, g_conv_out, w_out)
        x = x + memory_cross_attention(rms_norm(x, g_xattn), rms_norm(mem, g_mem),
                                       w_q_mem, w_kv_mem, w_o_mem)
        x = x + squared_relu_mlp(rms_norm(x, g_mlp), w_up, w_down)
    return rms_norm(x, g_final)
```

```python
import numpy as np
from contextlib import ExitStack
import concourse.bass as bass
import concourse.mybir as mybir
from concourse.bass_utils import run_bass_kernel_spmd

F32 = mybir.dt.float32
BF16 = mybir.dt.bfloat16
AF = mybir.ActivationFunctionType
ALU = mybir.AluOpType

S_LEN = 4096
D = 1024
MEM = 256
EPS = 1e-6
NW = 8
NT = 32


class Buf:
    __slots__ = ("w", "r", "name")

    def __init__(self, name=""):
        self.w = None
        self.r = []
        self.name = name


class Op:
    __slots__ = ("eng", "fn", "deps", "has_dep", "sem", "val", "is_dma", "idx")


class Sched:
    ENGS = ("pe", "act", "dve", "pool", "sp")
    N_DMA_SEMS = 40

    def __init__(self, nc):
        self.nc = nc
        self.ops = []
        self.last = {}
        self.dmas = []
        self.stopped = False

    def op(self, eng, fn, reads=(), writes=(), dma=False, extra_deps=(), nop=False, no_barrier=False):
        o = Op()
        if self.stopped:
            return None
        o.eng = eng
        o.fn = fn
        o.is_dma = dma
        o.has_dep = False
        o.sem = None
        o.val = 0
        o.idx = len(self.ops)
        deps = {}
        for b in reads:
            if b.w is not None:
                deps[b.w.idx] = b.w
        for b in writes:
            if b.w is not None:
                deps[b.w.idx] = b.w
            for r in b.r:
                deps[r.idx] = r
        for d in extra_deps:
            if d is not None:
                deps[d.idx] = d
        o.deps = list(deps.values())
        for b in reads:
            b.r.append(o)
        for b in writes:
            b.w = o
            b.r = []
        self.ops.append(o)
        if dma:
            if not no_barrier:
                self.dmas.append(o)
        elif not nop:
            self.last[eng] = o
        return o

    def barrier(self):
        if self.stopped:
            return
        deps = list(self.last.values()) + list(self.dmas)
        self.dmas = []
        for e in self.ENGS:
            self.op(e, lambda eng: None, extra_deps=deps, nop=True)

    def emit(self, stack):
        nc = self.nc

        def skip(d, o):
            return d.eng == "pe" and o.eng == "pe" and not d.is_dma and not o.is_dma

        for o in self.ops:
            for d in o.deps:
                if not skip(d, o):
                    d.has_dep = True
        esem = {e: stack.enter_context(nc.semaphore("sem_" + e)) for e in self.ENGS}
        dsems = [stack.enter_context(nc.semaphore("dsem%d" % i)) for i in range(self.N_DMA_SEMS)]
        dcnt = [0] * self.N_DMA_SEMS
        dlast = [None] * self.N_DMA_SEMS
        cnt = {e: 0 for e in self.ENGS}
        nd = 0
        for o in self.ops:
            if o.is_dma and o.eng == "pool":
                o.sem = stack.enter_context(nc.semaphore("swsem%d" % nd))
                o.val = 16
                nd += 1
            elif o.is_dma:
                i = nd % self.N_DMA_SEMS
                nd += 1
                if dlast[i] is not None:
                    o.deps.append(dlast[i])
                dcnt[i] += 16
                o.sem = dsems[i]
                o.val = dcnt[i]
                dlast[i] = o
            elif o.has_dep:
                cnt[o.eng] += 1
                o.sem = esem[o.eng]
                o.val = cnt[o.eng]
        progs = {e: [] for e in self.ENGS}
        waited = {e: {} for e in self.ENGS}
        nwait = 0
        for o in self.ops:
            w = waited[o.eng]
            waits = {}
            for d in o.deps:
                if skip(d, o):
                    continue
                key = id(d.sem)
                if w.get(key, 0) >= d.val:
                    continue
                w[key] = d.val
                waits[key] = (d.sem, d.val)
            nwait += len(waits)
            progs[o.eng].append((list(waits.values()), o))
        self.stats = dict(n_ops=len(self.ops), n_waits=nwait, counts=dict(cnt), n_dma=nd)

        def replay(eng_name):
            def body(eng):
                for waits, o in progs[eng_name]:
                    for (s, v) in waits:
                        eng.wait_ge(s, v)
                    ins = o.fn(eng)
                    if ins is not None and o.sem is not None:
                        ins.then_inc(o.sem, 16 if o.is_dma else 1)
            return body

        with nc.Block() as block:
            block.tensor(replay("pe"))
            block.scalar(replay("act"))
            block.vector(replay("dve"))
            block.gpsimd(replay("pool"))
            block.sync(replay("sp"))


def ts(start, count, step=1):
    return slice(start, start + (count - 1) * step + 1, step)


class Ring:
    def __init__(self, items):
        self.items = items
        self.i = 0

    def next(self):
        it = self.items[self.i % len(self.items)]
        self.i += 1
        return it


class _Stop(Exception):
    pass


def build_nc(dbg=None):
    dbg = dbg or {}
    nc = bass.Bass("TRN2", target_bir_lowering=False)
    dumps = []

    def din(name, shape):
        return nc.dram_tensor(name, list(shape), F32, kind="ExternalInput").ap()

    x = din("x", (S_LEN, D))
    mem = din("mem", (MEM, D))
    g_mix = din("g_mix", (D,))
    w_in = din("w_in", (D, 3072))
    conv_w = din("conv_w", (3, 512))
    g_attn_out = din("g_attn_out", (512,))
    g_conv_out = din("g_conv_out", (512,))
    w_out = din("w_out", (D, D))
    g_xattn = din("g_xattn", (D,))
    g_mem = din("g_mem", (D,))
    w_q_mem = din("w_q_mem", (D, D))
    w_kv_mem = din("w_kv_mem", (D, 2 * D))
    w_o_mem = din("w_o_mem", (D, D))
    g_mlp = din("g_mlp", (D,))
    w_up = din("w_up", (D, 4 * D))
    w_down = din("w_down", (4 * D, D))
    g_final = din("g_final", (D,))
    y = nc.dram_tensor("y", [S_LEN, D], F32, kind="ExternalOutput").ap()

    def stash(name, shape):
        return nc.dram_tensor(name, list(shape), BF16).ap()

    s_win = stash("s_win", (3, 128, 4096))
    s_wcv = stash("s_wcv", (4, 128, 3072))
    s_wout = stash("s_wout", (2, 128, 4096))
    s_wq = stash("s_wq", (2, 128, 4096))
    s_wkv = stash("s_wkv", (4, 128, 4096))
    s_wo = stash("s_wo", (2, 128, 4096))
    s_wup = stash("s_wup", (8, 128, 4096))
    s_wdn = stash("s_wdn", (8, 128, 4096))

    S = Sched(nc)

    def dump(name, src_ap, shape, dt, reads):
        if S.stopped:
            return
        d = nc.dram_tensor(name, list(shape), dt, kind="ExternalOutput").ap()
        dumps.append(S.op("sp", lambda e: e.dma_start(out=d, in_=src_ap), reads=reads, dma=True))

    def stop():
        S.op("sp", lambda e: None, extra_deps=dumps, nop=True)
        S.stopped = True
    with ExitStack() as st:
        try:
            def sb(name, shape, dt, stack=None):
                return (stack or st).enter_context(nc.sbuf_tensor(name, list(shape), dt))

            pall = [st.enter_context(nc.psum_tensor("pb%d" % i, [128, 512], F32)) for i in range(8)]
            Bpall = [Buf("pb%d" % i) for i in range(8)]
            pall_bf = [p[:].bitcast(BF16) for p in pall]
            pf = pall[2:8]
            Bpf = Bpall[2:8]
            tp_state = {"ring": Ring([(pall_bf[0], Bpall[0]), (pall_bf[1], Bpall[1])])}

            ident = sb("ident", [128, 128], BF16)
            ones_bf = sb("ones_bf", [128, 128], BF16)
            mask4 = sb("mask4", [128, 512], BF16)
            attnT = sb("attnT", [128, 4, S_LEN], BF16)
            ga = sb("ga", [128, 4], F32)
            gc = sb("gc", [128, 4], F32)
            cw = sb("cw", [128, 3, 4], F32)
            rstd_a = sb("rstd_a", [128, NT], F32)
            ssv = sb("ssv", [128, 8], F32)
            rsv = sb("rsv", [128, 8], F32)
            Bconst = Buf("const")
            BattnT = [[Buf("attnT%d_%d" % (p, w)) for w in range(NW)] for p in range(4)]
            Brstd_a = Buf("rstd_a")
            stat_ring = Ring([(i, Buf("ss%d" % i), Buf("rs%d" % i)) for i in range(8)])


            stash_ops = {k: [] for k in ("win0", "win1", "win2", "win", "wout", "wq", "wkv", "wo", "wup", "wdn")}

            def stash_cols(dst_chunk, src, c0, ncol, key):
                kk = src.shape[0] // 128
                S.op("pool", lambda e: e.dma_start(out=dst_chunk.rearrange("p (k c) -> p k c", k=kk),
                                                   in_=src[:, c0:c0 + ncol].rearrange("(k p) c -> p k c", p=128)),
                     writes=[], dma=True, no_barrier=True)
                stash_ops[key].append(S.ops[-1])

            for ci in range(3):
                stash_cols(s_win[ci], w_in, ci * 512, 512, "win%d" % ci)

            def stash_conv():
                for j in range(4):
                    for g in range(3):
                        S.op("pool", lambda e, j=j, g=g: e.dma_start(
                            out=s_wcv[j].rearrange("p (k g c) -> p k g c", k=8, g=3)[:, :, g, :],
                            in_=w_in[:, 1536 + g * 512 + j * 128:1536 + g * 512 + (j + 1) * 128].rearrange("(k p) c -> p k c", p=128)),
                            writes=[], dma=True, no_barrier=True)
                        stash_ops["win"].append(S.ops[-1])

            def stash_rows(dst_chunk, src, r0, nrow, key):
                kk = nrow // 128
                S.op("pool", lambda e: e.dma_start(out=dst_chunk.rearrange("p (k c) -> p k c", k=kk),
                                                   in_=src[r0:r0 + nrow, :].rearrange("(k p) c -> p k c", p=128)),
                     writes=[], dma=True, no_barrier=True)
                stash_ops[key].append(S.ops[-1])

            def mk_consts(e):
                e.memset(ident[:], 0.0)
                e.affine_select(out=ident[:], in_=ident[:], compare_op=ALU.not_equal, fill=1.0,
                                base=0, pattern=[[-1, 128]], channel_multiplier=1)
                e.memset(ones_bf[:], 1.0)
                e.memset(mask4[:], 1.0)
                for blk in range(4):
                    if blk % 2 == 0:
                        e.affine_select(out=mask4[:, blk * 128:(blk + 1) * 128], in_=mask4[:, blk * 128:(blk + 1) * 128],
                                        compare_op=ALU.is_ge, fill=0.0, base=0, pattern=[[-1, 128]], channel_multiplier=1)
                    else:
                        e.affine_select(out=mask4[:, blk * 128:(blk + 1) * 128], in_=mask4[:, blk * 128:(blk + 1) * 128],
                                        compare_op=ALU.is_ge, fill=0.0, base=0, pattern=[[1, 128]], channel_multiplier=-1)
                return None

            S.op("pool", lambda e: e.memset(ident[:], 0.0), writes=[Bconst])
            S.op("pool", lambda e: e.affine_select(out=ident[:], in_=ident[:], compare_op=ALU.not_equal, fill=1.0,
                                                    base=0, pattern=[[-1, 128]], channel_multiplier=1),
                 reads=[Bconst], writes=[Bconst])
            S.op("pool", lambda e: e.memset(ones_bf[:], 1.0), writes=[Bconst])
            S.op("pool", lambda e: e.memset(mask4[:], 1.0), writes=[Bconst])
            for blk in range(4):
                sl = slice(blk * 128, (blk + 1) * 128)
                if blk % 2 == 1:
                    S.op("pool", lambda e, sl=sl: e.affine_select(out=mask4[:, sl], in_=mask4[:, sl], compare_op=ALU.is_ge,
                                                                 fill=0.0, base=0, pattern=[[-1, 128]], channel_multiplier=1),
                         reads=[Bconst], writes=[Bconst])
                else:
                    S.op("pool", lambda e, sl=sl: e.affine_select(out=mask4[:, sl], in_=mask4[:, sl], compare_op=ALU.is_ge,
                                                                 fill=0.0, base=0, pattern=[[1, 128]], channel_multiplier=-1),
                         reads=[Bconst], writes=[Bconst])
            with nc.allow_non_contiguous_dma(reason="tiny gain / conv weight loads"):
                Bgvec = Buf("gvecs")
                Bgcv = Buf("gcv")
                Bcw = [Buf("cw%d" % t) for t in range(3)]
                S.op("act", lambda e: e.dma_start(out=ga[:], in_=g_attn_out.rearrange("(c p) -> p c", p=128), allow_slow_non_contiguous=True),
                     writes=[Bgvec], dma=True)
                S.op("act", lambda e: e.dma_start(out=gc[:], in_=g_conv_out.rearrange("(c p) -> p c", p=128), allow_slow_non_contiguous=True),
                     writes=[Bgcv], dma=True)
                for t in range(3):
                    S.op("act", lambda e, t=t: e.dma_start(out=cw[:, t, :], in_=conv_w[t].rearrange("(c p) -> p c", p=128), allow_slow_non_contiguous=True),
                         writes=[Bcw[t]], dma=True)

            evac_flip = [0]

            def evac(out_ap, in_ap, reads, writes):
                evac_flip[0] ^= 1
                if evac_flip[0]:
                    return S.op("act", lambda e: e.activation(out=out_ap, in_=in_ap, func=AF.Copy), reads=reads, writes=writes)
                return S.op("dve", lambda e: e.tensor_copy(out=out_ap, in_=in_ap), reads=reads, writes=writes)

            def mm_group(out_ap, pairs, reads, bank_buf, first_start=True):
                def fn(e):
                    ins = None
                    n = len(pairs)
                    for i, (l, r) in enumerate(pairs):
                        ins = e.matmul(out_ap, lhsT=l, rhs=r, start=(first_start and i == 0), stop=(i == n - 1),
                                       skip_group_check=True)
                    return ins
                return S.op("pe", fn, reads=reads, writes=[bank_buf])

            def rms_rstd(src_ap, Bsrc, n_feat, scr, Bscr):
                i, Bss, Brs = stat_ring.next()
                S.op("act", lambda e: e.activation(out=scr[:, 0:n_feat], in_=src_ap, func=AF.Square, accum_out=ssv[:, i:i + 1]),
                     reads=[Bsrc], writes=[Bscr, Bss])
                S.op("act", lambda e: e.activation(out=rsv[:, i:i + 1], in_=ssv[:, i:i + 1], func=AF.Ln, scale=1.0 / n_feat, bias=EPS),
                     reads=[Bss], writes=[Brs])
                S.op("act", lambda e: e.activation(out=rsv[:, i:i + 1], in_=rsv[:, i:i + 1], func=AF.Exp, scale=-0.5),
                     reads=[Brs], writes=[Brs])
                return rsv[:, i:i + 1], Brs

            def norm_front(src_ap, Bsrc, gb_ap, Bgb, hb, Bhb):
                rs, Brs = rms_rstd(src_ap, Bsrc, D, hb, Bhb)
                S.op("dve", lambda e: e.scalar_tensor_tensor(out=hb[:], in0=src_ap, scalar=rs, in1=gb_ap,
                                                              op0=ALU.mult, op1=ALU.mult),
                     reads=[Bsrc, Brs, Bgb], writes=[Bhb])

            def norm_to_T(src_ap, Bsrc, gb_ap, Bgb, hb, Bhb, dstT, BdstT, col0):
                norm_front(src_ap, Bsrc, gb_ap, Bgb, hb, Bhb)
                norm_T(hb, Bhb, dstT, BdstT, col0)

            def norm_T(hb, Bhb, dstT, BdstT, col0):
                pt, Bpt = tp_state["ring"].next()

                def tr(e):
                    ins = None
                    for kc in range(8):
                        ins = e.transpose(out=pt[:, kc * 128:(kc + 1) * 128], in_=hb[:, kc * 128:(kc + 1) * 128], identity=ident[:])
                    return ins
                S.op("pe", tr, reads=[Bhb, Bconst], writes=[Bpt])
                evac(dstT[:, :, col0:col0 + 128], pt[:].rearrange("p (k t) -> p k t", k=8), reads=[Bpt], writes=[BdstT])

            with ExitStack() as p1:
                KTA = sb("KTA", [128, 4, S_LEN], BF16, p1)
                VTA = sb("VTA", [128, 4, S_LEN], BF16, p1)
                BK = [[Buf("K%d_%d" % (p, w)) for w in range(NW)] for p in range(4)]
                BV = [[Buf("V%d_%d" % (p, w)) for w in range(NW)] for p in range(4)]
                BQ = BattnT

                with ExitStack() as p1a:
                    gbm = sb("gbm", [128, D], F32, p1a)
                    Bgbm = Buf("gbm")
                    S.op("sp", lambda e: e.dma_start(out=gbm[:], in_=g_mix.partition_broadcast(128)), writes=[Bgbm], dma=True)
                    win_sb = sb("win_sb", [128, 8, 1536], BF16, p1a)
                    Bwin = [Buf("win%d" % i) for i in range(3)]
                    xw = [sb("xw%d" % i, [128, 4, D], F32, p1a) for i in range(2)]
                    Bxw = [[Buf("xw%d_%d" % (i, s4)) for s4 in range(4)] for i in range(2)]
                    hbs = [[sb("hbs%d_%d" % (i, s4), [128, D], BF16, p1a) for s4 in range(4)] for i in range(2)]
                    Bhbs = [[Buf("hbs%d_%d" % (i, s4)) for s4 in range(4)] for i in range(2)]
                    hTw1 = [sb("hTw1_%d" % i, [128, 8, 512], BF16, p1a) for i in range(2)]
                    BhTw1 = [Buf("hTw1_%d" % i) for i in range(2)]
                    proj_ring = Ring(list(zip(pf, Bpf)))

                    def front1(w):
                        b = w % 2
                        S.op("sp", lambda e: e.dma_start(
                            out=xw[b][:], in_=x[w * 512:(w + 1) * 512, :].rearrange("(s p) d -> p s d", p=128)),
                            writes=Bxw[b], dma=True)
                        for s4 in range(4):
                            norm_front(xw[b][:, s4, :], Bxw[b][s4], gbm[:], Bgbm, hbs[b][s4], Bhbs[b][s4])

                    def T1(w):
                        b = w % 2
                        for s4 in range(4):
                            norm_T(hbs[b][s4], Bhbs[b][s4], hTw1[b], BhTw1[b], s4 * 128)

                    def proj1(w):
                        b = w % 2
                        for ci, (dst, Bdst) in enumerate(((attnT, BQ), (KTA, BK), (VTA, BV))):
                            for p in range(4):
                                c0 = ci * 512 + p * 128
                                bk, Bbk = proj_ring.next()
                                mm_group(bk[:], [(win_sb[:, kc, c0:c0 + 128], hTw1[b][:, kc, :]) for kc in range(8)],
                                         reads=[Bwin[ci], BhTw1[b]], bank_buf=Bbk)
                                evac(dst[:, p, w * 512:(w + 1) * 512], bk[:], reads=[Bbk], writes=[Bdst[p][w]])

                    front1(0)
                    front1(1)
                    for ci in range(3):
                        S.op("sp", lambda e, ci=ci: e.dma_start(
                            out=win_sb[:, :, ci * 512:(ci + 1) * 512],
                            in_=s_win[ci].rearrange("p (k c) -> p k c", k=8)),
                            writes=[Bwin[ci]], dma=True, extra_deps=stash_ops["win%d" % ci])
                    T1(0)
                    for w in range(NW):
                        if w + 2 < NW:
                            front1(w + 2)
                        if w + 1 < NW:
                            T1(w + 1)
                        proj1(w)
                    S.barrier()

                with ExitStack() as p1c:
                    Vp = {d: sb("Vp%d" % d, [128, NT, 192], BF16, p1c) for d in (1, 4, 16)}
                    BVp = {d: [Buf("Vp%d_%d" % (d, i)) for i in range(NT)] for d in (1, 4, 16)}
                    QT4 = sb("QT4", [128, NW, 4, 128], BF16, p1c)
                    KT4 = sb("KT4", [128, NW, 4, 128], BF16, p1c)
                    QT16 = sb("QT16", [128, 2, 16, 128], BF16, p1c)
                    KT16 = sb("KT16", [128, 2, 16, 128], BF16, p1c)
                    BQT4, BKT4, BQT16, BKT16 = Buf("QT4"), Buf("KT4"), Buf("QT16"), Buf("KT16")
                    P1 = [sb("P1_%d" % i, [128, 4, 2, 128], BF16, p1c) for i in range(3)]
                    P4 = [sb("P4_%d" % i, [128, 4, 2, 128], BF16, p1c) for i in range(3)]
                    BP1 = [[Buf("P1_%d_%d" % (i, k)) for k in range(2)] for i in range(3)]
                    BP4 = [[Buf("P4_%d_%d" % (i, k)) for k in range(2)] for i in range(3)]
                    P16b = sb("P16b", [128, 16, 128], BF16, p1c)
                    BP16b = [Buf("P16b_%d" % k) for k in range(8)]
                    tmpn = [sb("tmpn%d" % i, [128, 512], F32, p1c) for i in range(2)]
                    Btmpn = [Buf("tmpn%d" % i) for i in range(2)]
                    recn = [sb("recn%d" % i, [128, 512], F32, p1c) for i in range(2)]
                    Brecn = [Buf("recn%d" % i) for i in range(2)]
                    sqp = sb("sqp", [128, S_LEN], BF16, p1c)
                    Bsqp = [Buf("sqp%d" % w) for w in range(NW)]
                    tp_state["ring"] = Ring([(pall_bf[0], Bpall[0]), (pall_bf[7], Bpall[7])])
                    S_ring = Ring([(pall[1], Bpall[1]), (pall[2], Bpall[2]), (pall[3], Bpall[3])])
                    O_ring = Ring([(pall[4], Bpall[4]), (pall[5], Bpall[5])])
                    ssq_bank, Bssq = pall[6], Bpall[6]

                    for d in (1, 4, 16):
                        S.op("pool", lambda e, d=d: e.memset(Vp[d][:, :, 64:128], 1.0), writes=BVp[d])

                    ssq_started = [False]
                    mflip = [0]

                    def mask_mul(dst, msk, Bd):
                        S.op("dve", lambda e: e.tensor_tensor(out=dst, in0=dst, in1=msk, op=ALU.mult),
                             reads=[Bd, Bconst], writes=[Bd])

                    def do_pair(pair):
                        VTp = VTA[:, pair, :]
                        P16a = VTp.rearrange("p (r a b) -> p r a b", r=16, a=2)
                        BP16a = BV[pair]
                        for d in (1, 4, 16):
                            for t0 in range(0, NT, 8):
                                pt, Bpt = tp_state["ring"].next()

                                def trv(e, d=d, t0=t0, pt=pt):
                                    ins = None
                                    for j in range(8):
                                        t = t0 + j
                                        if d == 1:
                                            src = VTp[:, t * 128:(t + 1) * 128]
                                        elif d == 4:
                                            src = VTp[:, ts(512 * (t // 4) + (t % 4), 128, 4)]
                                        else:
                                            src = VTp[:, ts(2048 * (t // 16) + (t % 16), 128, 16)]
                                        ins = e.transpose(out=pt[:, j * 128:(j + 1) * 128], in_=src, identity=ident[:])
                                    return ins
                                S.op("pe", trv, reads=BV[pair] + [Bconst], writes=[Bpt])
                                evac(Vp[d][:, t0:t0 + 8, :].rearrange("p t (b e) -> p t b e", b=3)[:, :, 0:3:2, :],
                                     pt[:].rearrange("p (t b e) -> p t b e", t=8, b=2),
                                     reads=[Bpt], writes=BVp[d][t0:t0 + 8])
                        Qp, Kp = attnT[:, pair, :], KTA[:, pair, :]
                        S.op("dve", lambda e, Qp=Qp: e.tensor_copy(out=QT16[:], in_=Qp.rearrange("p (n i r) -> p n r i", r=16, i=128)),
                             reads=BQ[pair], writes=[BQT16])
                        S.op("act", lambda e, Kp=Kp: e.activation(out=KT16[:], in_=Kp.rearrange("p (n i r) -> p n r i", r=16, i=128), func=AF.Copy),
                             reads=BK[pair], writes=[BKT16])
                        S.op("dve", lambda e, Qp=Qp: e.tensor_copy(out=QT4[:], in_=Qp.rearrange("p (n i r) -> p n r i", r=4, i=128)),
                             reads=BQ[pair], writes=[BQT4])
                        S.op("act", lambda e, Kp=Kp: e.activation(out=KT4[:], in_=Kp.rearrange("p (n i r) -> p n r i", r=4, i=128), func=AF.Copy),
                             reads=BK[pair], writes=[BKT4])

                        if dbg.get("stop") == "proj" and pair == 0:
                            dump("d_QT", attnT[:, 0, :], [128, S_LEN], BF16, BQ[0])
                            dump("d_KT", KTA[:, 0, :], [128, S_LEN], BF16, BK[0])
                            dump("d_QT16", QT16[:], [128, 2, 16, 128], BF16, [BQT16])
                            dump("d_KT4", KT4[:], [128, NW, 4, 128], BF16, [BKT4])
                            for d in (1, 4, 16):
                                dump("d_Vp%d" % d, Vp[d][:], [128, NT, 192], BF16, BVp[d])
                            stop()

                        def score_tiles(hh, tiles):
                            for i in range(0, len(tiles), 2):
                                chunk = tiles[i:i + 2]
                                sbank, Bsb = S_ring.next()
                                mms = []
                                rds = []
                                for j, (kT, q, hn, Pd, Bd, rd, _pd) in enumerate(chunk):
                                    n = 256 if hn else 128
                                    mms.append((sbank[:, j * 256:j * 256 + n], kT, q))
                                    rds += rd

                                def fn(e, mms=mms):
                                    ins = None
                                    for k, (o, l, r) in enumerate(mms):
                                        ins = e.matmul(o, lhsT=l, rhs=r, start=(k == 0), stop=(k == len(mms) - 1),
                                                       skip_group_check=True)
                                    return ins
                                S.op("pe", fn, reads=rds, writes=[Bsb])
                                if len(chunk) == 2 and chunk[0][2] and chunk[1][2] and chunk[0][4] is chunk[1][4] and dbg.get("fuse", 1):
                                    Pd0 = chunk[0][3]
                                    Bd = chunk[0][4]
                                    dst = chunk[0][6]
                                    S.op("act", lambda e, dst=dst, sbank=sbank: e.activation(out=dst, in_=sbank[:], func=AF.Exp, scale=0.125),
                                         reads=[Bsb], writes=[Bd])
                                    mask_mul(dst, mask4[:], Bd)
                                else:
                                    for j, (kT, q, hn, Pd, Bd, rd, _pd) in enumerate(chunk[:2]):
                                        n = 256 if hn else 128
                                        src = sbank[:, j * 256:j * 256 + n]
                                        dst = Pd.rearrange("p a b -> p (a b)") if hn else Pd
                                        S.op("act", lambda e, dst=dst, src=src: e.activation(out=dst, in_=src, func=AF.Exp, scale=0.125),
                                             reads=[Bsb], writes=[Bd])
                                        mask_mul(dst, mask4[:, 0:n], Bd)

                        def S16(hh, n2):
                            hp = slice(64 * hh, 64 * hh + 64)
                            tiles = []
                            for r in range(16):
                                kT = KT16[hp, n2, r, :]
                                if n2 == 0:
                                    q = QT16[hp, 0:2, r, :]
                                    Pd = P16a[:, r, :, :]
                                    Bd = BP16a[r // 2]
                                    pairdst = P16a[:, r - 1:r + 1, :, :].rearrange("p r a b -> p (r a b)") if r % 2 == 1 else None
                                    tiles.append([kT, q, True, Pd, Bd, [BKT16, BQT16], pairdst])
                                else:
                                    q = QT16[hp, 1, r, :]
                                    tiles.append([kT, q, False, P16b[:, r, :], BP16b[r // 2], [BKT16, BQT16], None])
                            for i in range(0, 16, 2):
                                tiles[i][6] = tiles[i + 1][6]
                            score_tiles(hh, [tuple(t) for t in tiles])

                        def S1(hh, w):
                            hp = slice(64 * hh, 64 * hh + 64)
                            b = w % 3
                            tiles = []
                            for g in range(4):
                                t = 4 * w + g
                                hn = t + 1 < NT
                                kT = KTA[hp, pair, 128 * t:128 * t + 128]
                                q = attnT[hp, pair, 128 * t:128 * t + (256 if hn else 128)]
                                rd = [BK[pair][w], BQ[pair][w]] + ([BQ[pair][w + 1]] if (g == 3 and hn) else [])
                                Pd = P1[b][:, g, :, :] if hn else P1[b][:, g, 0, :]
                                pairdst = P1[b][:, g - 1:g + 1, :, :].rearrange("p r a b -> p (r a b)") if g % 2 == 1 else None
                                tiles.append((kT, q, hn, Pd, BP1[b][g // 2], rd, pairdst))
                            tiles = [(t[0], t[1], t[2], t[3], t[4], t[5], tiles[(i // 2) * 2 + 1][6]) for i, t in enumerate(tiles)]
                            score_tiles(hh, tiles)

                        def S4(hh, w):
                            hp = slice(64 * hh, 64 * hh + 64)
                            b = w % 3
                            hn = w + 1 < NW
                            tiles = []
                            for r in range(4):
                                kT = KT4[hp, w, r, :]
                                q = QT4[hp, w:w + 2, r, :] if hn else QT4[hp, w, r, :]
                                Pd = P4[b][:, r, :, :] if hn else P4[b][:, r, 0, :]
                                pairdst = P4[b][:, r - 1:r + 1, :, :].rearrange("p r a b -> p (r a b)") if r % 2 == 1 else None
                                tiles.append((kT, q, hn, Pd, BP4[b][r // 2], [BKT4, BQT4], pairdst))
                            tiles = [(t[0], t[1], t[2], t[3], t[4], t[5], tiles[(i // 2) * 2 + 1][6]) for i, t in enumerate(tiles)]
                            score_tiles(hh, tiles)

                        def PV_norm(hh, w):
                            vsl = slice(64 * hh, 64 * hh + 128)
                            nump = slice(64 * hh, 64 * hh + 64)
                            denp = slice(64 * (1 - hh), 64 * (1 - hh) + 64)
                            n2, ww = w // 4, w % 4
                            b, pb = w % 3, (w - 1) % 3
                            ob, Bob = O_ring.next()
                            pv = []
                            for g in range(4):
                                t = 4 * w + g
                                pv.append((ob[:, g * 128:(g + 1) * 128], Vp[1][:, t, vsl], P1[b][:, g, 0, :]))
                                if t >= 1:
                                    prevP = P1[b][:, g - 1, 1, :] if g >= 1 else P1[pb][:, 3, 1, :]
                                    pv.append((ob[:, g * 128:(g + 1) * 128], Vp[1][:, t - 1, vsl], prevP))
                            for r in range(4):
                                pv.append((ob[:, ts(r, 128, 4)], Vp[4][:, 4 * w + r, vsl], P4[b][:, r, 0, :]))
                                if w >= 1:
                                    pv.append((ob[:, ts(r, 128, 4)], Vp[4][:, 4 * (w - 1) + r, vsl], P4[pb][:, r, 1, :]))
                            for r in range(16):
                                csl = slice(32 * ww, 32 * ww + 32)
                                if n2 == 0:
                                    pv.append((ob[:, ts(r, 32, 16)], Vp[16][:, r, vsl], P16a[:, r, 0, csl]))
                                else:
                                    pv.append((ob[:, ts(r, 32, 16)], Vp[16][:, 16 + r, vsl], P16b[:, r, csl]))
                                    pv.append((ob[:, ts(r, 32, 16)], Vp[16][:, r, vsl], P16a[:, r, 1, csl]))

                            def pvfn(e, pv=pv):
                                ins = None
                                for k, (o, l, r) in enumerate(pv):
                                    ins = e.matmul(o, lhsT=l, rhs=r, start=(k == 0), stop=(k == len(pv) - 1),
                                                   skip_group_check=True)
                                return ins
                            S.op("pe", pvfn, reads=BP1[b] + BP1[pb] + BP4[b] + BP4[pb] + BP16a + (BP16b if n2 == 1 else []) + BVp[1] + BVp[4] + BVp[16],
                                 writes=[Bob])
                            nb = (hh * NW + w) % 2
                            S.op("act", lambda e: e.activation(out=recn[nb][denp, :], in_=ob[denp, :], func=AF.Ln),
                                 reads=[Bob], writes=[Brecn[nb]])
                            S.op("act", lambda e: e.activation(out=recn[nb][denp, :], in_=recn[nb][denp, :], func=AF.Exp, scale=-1.0),
                                 reads=[Brecn[nb]], writes=[Brecn[nb]])
                            S.op("dve", lambda e: e.tensor_tensor(out=tmpn[nb][nump, :], in0=ob[nump, :], in1=recn[nb][denp, :], op=ALU.mult),
                                 reads=[Bob, Brecn[nb]], writes=[Btmpn[nb]])
                            S.op("pool", lambda e: e.tensor_tensor(out=sqp[nump, w * 512:(w + 1) * 512], in0=tmpn[nb][nump, :],
                                                                   in1=tmpn[nb][nump, :], op=ALU.mult),
                                 reads=[Btmpn[nb]], writes=[Bsqp[w]])
                            S.op("pool", lambda e: e.tensor_scalar(
                                out=attnT[nump, pair, w * 512:(w + 1) * 512], in0=tmpn[nb][nump, :],
                                scalar1=ga[nump, pair:pair + 1], scalar2=1.0, op0=ALU.mult, op1=ALU.mult),
                                reads=[Btmpn[nb], Bgvec], writes=[BattnT[pair][w]])

                        for hh in range(2):
                            S16(hh, 0)
                            S1(hh, 0)
                            S4(hh, 0)
                            for w in range(NW):
                                if w + 1 < NW:
                                    if w + 1 == 4:
                                        S16(hh, 1)
                                    S1(hh, w + 1)
                                    S4(hh, w + 1)
                                PV_norm(hh, w)
                        for w in range(NW):
                            def ssfn(e, w=w, first=not ssq_started[0]):
                                ins = None
                                for s4 in range(4):
                                    tt = 4 * w + s4
                                    ins = e.matmul(ssq_bank[:, tt:tt + 1], lhsT=sqp[:, tt * 128:(tt + 1) * 128], rhs=ones_bf[:, 0:1],
                                                   start=(first and s4 == 0), stop=True, skip_group_check=True)
                                return ins
                            S.op("pe", ssfn, reads=[Bsqp[w], Bconst], writes=[Bssq])
                            ssq_started[0] = True
                        if dbg.get("stop") == "attn0" and pair == 0:
                            dump("d_attnT", attnT[:, 0, :], [128, S_LEN], BF16, BattnT[0])
                            stop()
                        if not dbg.get("stash2", True):
                            pass
                        elif pair == 0:
                            stash_conv()
                            for c in range(4):
                                stash_cols(s_wkv[c], w_kv_mem, c * 512, 512, "wkv")
                            for c in range(2):
                                stash_cols(s_wout[c], w_out, c * 512, 512, "wout")
                            for c in range(2):
                                stash_cols(s_wq[c], w_q_mem, c * 512, 512, "wq")
                            for c in range(2):
                                stash_cols(s_wo[c], w_o_mem, c * 512, 512, "wo")
                        elif pair == 1:
                            for c in range(8):
                                stash_cols(s_wup[c], w_up, c * 512, 512, "wup")
                        elif pair == 2:
                            for c in range(8):
                                stash_rows(s_wdn[c], w_down, c * 512, 512, "wdn")

                    for pair_i in range(dbg.get("pairs", 4)):
                        do_pair(pair_i)

                    S.op("act", lambda e: e.activation(out=rstd_a[:], in_=ssq_bank[:, 0:NT], func=AF.Ln, scale=1.0 / 512, bias=EPS),
                         reads=[Bssq], writes=[Brstd_a])
                    S.op("act", lambda e: e.activation(out=rstd_a[:], in_=rstd_a[:], func=AF.Exp, scale=-0.5),
                         reads=[Brstd_a], writes=[Brstd_a])
                    S.barrier()
                    tp_state["ring"] = Ring([(pall_bf[0], Bpall[0]), (pall_bf[1], Bpall[1])])
                    if dbg.get("stop") == "1c":
                        dump("d_attnT", attnT[:], [128, 4, S_LEN], BF16, sum(BattnT, []))
                        dump("d_rstd_a", rstd_a[:], [128, NT], F32, [Brstd_a])
                        stop()

            with ExitStack() as p2:
                gb4 = sb("gb4", [128, 4, D], F32, p2)
                Bgb4 = Buf("gb4")
                for i, gvec in enumerate((g_mix, g_xattn, g_mlp, g_final)):
                    S.op("sp", lambda e, i=i, gvec=gvec: e.dma_start(out=gb4[:, i, :], in_=gvec.partition_broadcast(128)),
                         writes=[Bgb4], dma=True)
                memKT = sb("memKT", [128, 8, MEM], BF16, p2)
                memV = sb("memV", [128, 2, D], BF16, p2)
                BmemKT, BmemV = Buf("memKT"), Buf("memV")
                uw = sb("uw", [128, 514], F32, p2)
                Buw = Buf("uw")
                ucar = sb("ucar", [128, 4, 2], F32, p2)
                Bucar = [Buf("ucar%d" % j) for j in range(4)]
                hTm = sb("hTm", [128, 8, 512], BF16, p2)
                BhTm = Buf("hTm")
                xr = [sb("xr%d" % i, [128, 4, D], F32, p2) for i in range(2)]
                Bxr = [[Buf("xr%d_%d" % (i, s4)) for s4 in range(4)] for i in range(2)]
                hTw = sb("hTw", [128, 8, 512], BF16, p2)
                BhTw = Buf("hTw")
                hb2 = [sb("hb2_%d" % i, [128, D], BF16, p2) for i in range(4)]
                Bhb2 = [Buf("hb2_%d" % i) for i in range(4)]
                hb3 = [sb("hb3_%d" % i, [128, D], BF16, p2) for i in range(4)]
                Bhb3 = [Buf("hb3_%d" % i) for i in range(4)]
                xcs = sb("xcs", [128, 512], F32, p2)
                acc = sb("acc", [128, 512], F32, p2)
                ycv = sb("ycv", [128, 512], F32, p2)
                Bxcs, Bacc, Bycv = Buf("xcs"), Buf("acc"), Buf("ycv")
                sqc = sb("sqc", [128, 4, 512], BF16, p2)
                convT = sb("convT", [128, 4, 512], BF16, p2)
                Bsqc, BconvT = Buf("sqc"), Buf("convT")
                rstd_c = sb("rstd_c", [128, 4], F32, p2)
                Brstd_c = Buf("rstd_c")
                NRING = 5
                ring_t = [sb("wring%d" % i, [128, 4096], BF16, p2) for i in range(NRING)]
                wring = Ring([(ring_t[i], Buf("wring%d" % i)) for i in range(NRING)])
                qT = sb("qT", [128, 8, 512], BF16, p2)
                BqT = Buf("qT")
                PX = [sb("PX%d" % i, [128, 2, 512], BF16, p2) for i in range(2)]
                BPX = [Buf("PX%d" % i) for i in range(2)]
                recx = [sb("recx%d" % i, [128, 512], F32, p2) for i in range(1)] * 2
                Brecx = [Buf("recx%d" % i) for i in range(1)] * 2
                oT = hTw
                BoT = BhTw
                rl = [sb("rl%d" % i, [128, 512], F32, p2) for i in range(2)]
                Brl = [Buf("rl%d" % i) for i in range(2)]
                hid = [sb("hid%d" % i, [128, 4, 512], BF16, p2) for i in range(2)]
                Bhid = [Buf("hid%d" % i) for i in range(2)]
                bank_ring = Ring(list(zip(pf, Bpf)))

                def wload(src_ap, a, b, dep_key):
                    t, Bt = wring.next()
                    view = t[:, 0:a * b].rearrange("p (a b) -> p a b", a=a)
                    S.op("sp", lambda e: e.dma_start(out=view, in_=src_ap.rearrange("p (a b) -> p a b", a=a)), writes=[Bt], dma=True,
                         extra_deps=stash_ops[dep_key])
                    return view, Bt

                S.op("pool", lambda e: e.memset(ucar[:], 0.0), writes=Bucar)

                with ExitStack() as pm:
                    mt_ = [xr[0][:, 0, :], xr[0][:, 1, :]]
                    Bmt = [Bxr[0][0], Bxr[0][1]]
                    gbmem = xr[0][:, 2, :]
                    Bgbmem = Bxr[0][2]
                    memhT = hTw[:, :, 0:MEM]
                    BmemhT = BhTw
                    S.op("sp", lambda e: e.dma_start(out=gbmem, in_=g_mem.partition_broadcast(128)), writes=[Bgbmem], dma=True)
                    for i in range(2):
                        S.op("sp", lambda e, i=i: e.dma_start(out=mt_[i], in_=mem[i * 128:(i + 1) * 128, :]), writes=[Bmt[i]], dma=True)
                        norm_to_T(mt_[i], Bmt[i], gbmem, Bgbmem, hb2[i], Bhb2[i], memhT, BmemhT, i * 128)
                    for half in range(2):
                        wv, Bw = wload(s_wkv[half], 8, 512, "wkv")
                        for cc in range(4):
                            c = half * 4 + cc
                            bk, Bbk = bank_ring.next()
                            mm_group(bk[:, 0:MEM], [(wv[:, kc, cc * 128:(cc + 1) * 128], memhT[:, kc, :]) for kc in range(8)],
                                     reads=[Bw, BmemhT], bank_buf=Bbk)
                            evac(memKT[:, c, :], bk[:, 0:MEM], reads=[Bbk], writes=[BmemKT])
                    for half in range(2):
                        wv, Bw = wload(s_wkv[2 + half], 8, 512, "wkv")
                        for mt in range(2):
                            bk, Bbk = bank_ring.next()
                            mm_group(bk[:], [(memhT[:, kc, mt * 128:(mt + 1) * 128], wv[:, kc, :]) for kc in range(8)],
                                     reads=[Bw, BmemhT], bank_buf=Bbk)
                            evac(memV[:, mt, half * 512:(half + 1) * 512], bk[:], reads=[Bbk], writes=[BmemV])

                out_ops = []

                def h1_steps(w):
                    xb = w % 2
                    X, BX = xr[xb], Bxr[xb]

                    def s_load():
                        S.op("sp", lambda e: e.dma_start(
                            out=X[:], in_=x[w * 512:(w + 1) * 512, :].rearrange("(s p) d -> p s d", p=128)),
                            writes=BX, dma=True)
                        for s4 in range(4):
                            norm_front(X[:, s4, :], BX[s4], gb4[:, 0, :], Bgb4, hb2[s4], Bhb2[s4])

                    def s_T0conv0():
                        for s4 in range(4):
                            norm_T(hb2[s4], Bhb2[s4], hTw, BhTw, s4 * 128)
                        s_conv(0)

                    def s_conv(j):
                        t, Bt = wring.next()
                        wv = t[:, 0:8 * 384].rearrange("p (a b) -> p a b", a=8)
                        S.op("sp", lambda e: e.dma_start(out=wv, in_=s_wcv[j].rearrange("p (a b) -> p a b", a=8)),
                             writes=[Bt], dma=True, extra_deps=stash_ops["win"])
                        banks = [bank_ring.next() for _ in range(3)]
                        for ci in range(3):
                            mm_group(banks[ci][0][:], [(wv[:, kc, ci * 128:(ci + 1) * 128], hTw[:, kc, :]) for kc in range(8)],
                                     reads=[Bt, BhTw], bank_buf=banks[ci][1])
                        (bgp, Bbg), (cgp, Bcg), (xcp, Bxc) = banks
                        S.op("pool", lambda e: e.tensor_copy(out=uw[:, 0:2], in_=ucar[:, j, :]), reads=[Bucar[j]], writes=[Buw])
                        S.op("act", lambda e: e.activation(out=xcs[:], in_=xcp[:], func=AF.Copy), reads=[Bxc], writes=[Bxcs])
                        S.op("act", lambda e: e.activation(out=uw[:, 2:514], in_=cgp[:], func=AF.Copy), reads=[Bcg], writes=[Buw])
                        S.op("act", lambda e: e.activation(out=ycv[:], in_=bgp[:], func=AF.Copy), reads=[Bbg], writes=[Bycv])
                        S.op("pool", lambda e: e.tensor_tensor(out=uw[:, 2:514], in0=uw[:, 2:514], in1=xcs[:], op=ALU.mult),
                             reads=[Buw, Bxcs], writes=[Buw])
                        S.op("pool", lambda e: e.tensor_scalar(out=acc[:], in0=uw[:, 2:514], scalar1=cw[:, 2, j:j + 1], scalar2=1.0,
                                                               op0=ALU.mult, op1=ALU.mult),
                             reads=[Buw] + Bcw, writes=[Bacc])
                        for tap, sl in ((1, slice(1, 513)), (0, slice(0, 512))):
                            S.op("pool", lambda e, tap=tap, sl=sl: e.tensor_scalar(out=xcs[:], in0=uw[:, sl], scalar1=cw[:, tap, j:j + 1], scalar2=1.0,
                                                                                   op0=ALU.mult, op1=ALU.mult),
                                 reads=[Buw] + Bcw, writes=[Bxcs])
                            S.op("pool", lambda e: e.tensor_tensor(out=acc[:], in0=acc[:], in1=xcs[:], op=ALU.add),
                                 reads=[Bacc, Bxcs], writes=[Bacc])
                        S.op("pool", lambda e: e.tensor_tensor(out=ycv[:], in0=ycv[:], in1=acc[:], op=ALU.mult),
                             reads=[Bycv, Bacc], writes=[Bycv])
                        S.op("act", lambda e: e.activation(out=sqc[:, j, :], in_=ycv[:], func=AF.Square), reads=[Bycv], writes=[Bsqc])
                        S.op("pool", lambda e: e.tensor_scalar(out=convT[:, j, :], in0=ycv[:], scalar1=gc[:, j:j + 1], scalar2=1.0,
                                                               op0=ALU.mult, op1=ALU.mult),
                             reads=[Bycv, Bgcv], writes=[BconvT])
                        S.op("pool", lambda e: e.tensor_copy(out=ucar[:, j, :], in_=uw[:, 512:514]), reads=[Buw], writes=[Bucar[j]])

                    def st_wout():
                        bk, Bbk = bank_ring.next()

                        def ssc(e):
                            ins = None
                            k = 0
                            for s4 in range(4):
                                for j in range(4):
                                    ins = e.matmul(bk[:, s4:s4 + 1], lhsT=sqc[:, j, s4 * 128:(s4 + 1) * 128], rhs=ones_bf[:, 0:1],
                                                   start=(k == 0), stop=(k == 15), skip_group_check=True)
                                    k += 1
                            return ins
                        S.op("pe", ssc, reads=[Bsqc, Bconst], writes=[Bbk])
                        S.op("act", lambda e: e.activation(out=rstd_c[:], in_=bk[:, 0:4], func=AF.Ln, scale=1.0 / 512, bias=EPS),
                             reads=[Bbk], writes=[Brstd_c])
                        S.op("act", lambda e: e.activation(out=rstd_c[:], in_=rstd_c[:], func=AF.Exp, scale=-0.5),
                             reads=[Brstd_c], writes=[Brstd_c])
                        wvs = [wload(s_wout[half], 8, 512, "wout")
                               for half in range(2)]
                        for s4 in range(4):
                            for half in range(2):
                                wv, Bw = wvs[half]
                                tsl = slice(w * 512 + s4 * 128, w * 512 + (s4 + 1) * 128)
                                pa, Bpa = bank_ring.next()
                                pc, Bpc = bank_ring.next()
                                mm_group(pa[:], [(attnT[:, kc, tsl], wv[:, kc, :]) for kc in range(4)],
                                         reads=[Bw] + [BattnT[p][w] for p in range(4)], bank_buf=Bpa)
                                mm_group(pc[:], [(convT[:, kc, s4 * 128:(s4 + 1) * 128], wv[:, 4 + kc, :]) for kc in range(4)],
                                         reads=[Bw, BconvT], bank_buf=Bpc)
                                xs = X[:, s4, half * 512:(half + 1) * 512]
                                tt = 4 * w + s4
                                S.op("dve", lambda e, pa=pa, xs=xs, tt=tt: e.scalar_tensor_tensor(
                                    out=xs, in0=pa[:], scalar=rstd_a[:, tt:tt + 1], in1=xs, op0=ALU.mult, op1=ALU.add),
                                    reads=[Bpa, Brstd_a, BX[s4]], writes=[BX[s4]])
                                S.op("dve", lambda e, pc=pc, xs=xs, s4=s4: e.scalar_tensor_tensor(
                                    out=xs, in0=pc[:], scalar=rstd_c[:, s4:s4 + 1], in1=xs, op0=ALU.mult, op1=ALU.add),
                                    reads=[Bpc, Brstd_c, BX[s4]], writes=[BX[s4]])
                            norm_front(X[:, s4, :], BX[s4], gb4[:, 1, :], Bgb4, hb2[s4], Bhb2[s4])

                    def s_q():
                        for s4 in range(4):
                            norm_T(hb2[s4], Bhb2[s4], hTw, BhTw, s4 * 128)
                        for half in range(2):
                            wv, Bw = wload(s_wq[half], 8, 512, "wq")
                            for cc in range(4):
                                c = half * 4 + cc
                                bk, Bbk = bank_ring.next()
                                mm_group(bk[:], [(wv[:, kc, cc * 128:(cc + 1) * 128], hTw[:, kc, :]) for kc in range(8)],
                                         reads=[Bw, BhTw], bank_buf=Bbk)
                                evac(qT[:, c, :], bk[:], reads=[Bbk], writes=[BqT])

                    def s_xattn():
                        for h in range(4):
                            pb = h % 2
                            for mt in range(2):
                                bk, Bbk = bank_ring.next()
                                mm_group(bk[:], [(memKT[:, 2 * h + cc, mt * 128:(mt + 1) * 128], qT[:, 2 * h + cc, :]) for cc in range(2)],
                                         reads=[BmemKT, BqT], bank_buf=Bbk)
                                S.op("act", lambda e, bk=bk, pb=pb, mt=mt: e.activation(out=PX[pb][:, mt, :], in_=bk[:], func=AF.Exp, scale=1.0 / 16),
                                     reads=[Bbk], writes=[BPX[pb]])
                            bk, Bbk = bank_ring.next()
                            mm_group(bk[:], [(ones_bf[:], PX[pb][:, mt, :]) for mt in range(2)], reads=[Bconst, BPX[pb]], bank_buf=Bbk)
                            S.op("act", lambda e, bk=bk, pb=pb: e.activation(out=recx[pb][:], in_=bk[:], func=AF.Ln), reads=[Bbk], writes=[Brecx[pb]])
                            S.op("act", lambda e, pb=pb: e.activation(out=recx[pb][:], in_=recx[pb][:], func=AF.Exp, scale=-1.0),
                                 reads=[Brecx[pb]], writes=[Brecx[pb]])
                            for cc in range(2):
                                c = 2 * h + cc
                                bk, Bbk = bank_ring.next()
                                mm_group(bk[:], [(memV[:, mt, c * 128:(c + 1) * 128], PX[pb][:, mt, :]) for mt in range(2)],
                                         reads=[BmemV, BPX[pb]], bank_buf=Bbk)
                                S.op("dve", lambda e, bk=bk, pb=pb, c=c: e.tensor_tensor(out=oT[:, c, :], in0=bk[:], in1=recx[pb][:], op=ALU.mult),
                                     reads=[Bbk, Brecx[pb]], writes=[BoT])

                    def st_wo():
                        wvs = [wload(s_wo[half], 8, 512, "wo")
                               for half in range(2)]
                        for s4 in range(4):
                            for half in range(2):
                                wv, Bw = wvs[half]
                                bk, Bbk = bank_ring.next()
                                mm_group(bk[:], [(oT[:, kc, s4 * 128:(s4 + 1) * 128], wv[:, kc, :]) for kc in range(8)],
                                         reads=[Bw, BoT], bank_buf=Bbk)
                                xs = X[:, s4, half * 512:(half + 1) * 512]
                                S.op("dve", lambda e, bk=bk, xs=xs: e.tensor_tensor(out=xs, in0=bk[:], in1=xs, op=ALU.add),
                                     reads=[Bbk, BX[s4]], writes=[BX[s4]])
                            norm_front(X[:, s4, :], BX[s4], gb4[:, 2, :], Bgb4, hb3[s4], Bhb3[s4])

                    return [s_load, s_T0conv0] + [(lambda j=j: s_conv(j)) for j in range(1, 4)] + [st_wout, s_q, s_xattn, st_wo]

                def h2_parts(w):
                    xb = w % 2
                    X, BX = xr[xb], Bxr[xb]


                    wds = {}

                    def U(e8):
                        wu, Bwu = wload(s_wup[e8], 8, 512, "wup")
                        hbuf, Bhbuf = hid[e8 % 2], Bhid[e8 % 2]
                        for hc in range(4):
                            bk, Bbk = bank_ring.next()
                            mm_group(bk[:], [(wu[:, kc, hc * 128:(hc + 1) * 128], hTm[:, kc, :]) for kc in range(8)],
                                     reads=[Bwu, BhTm], bank_buf=Bbk)
                            rb = hc % 2
                            S.op("act", lambda e, bk=bk, rb=rb: e.activation(out=rl[rb][:], in_=bk[:], func=AF.Relu), reads=[Bbk], writes=[Brl[rb]])
                            S.op("pool", lambda e, rb=rb, hc=hc: e.tensor_tensor(out=hbuf[:, hc, :], in0=rl[rb][:], in1=rl[rb][:], op=ALU.mult),
                                 reads=[Brl[rb]], writes=[Bhbuf])

                    def Dn(e8):
                        wd, Bwd = wload(s_wdn[e8], 4, 1024, "wdn")
                        hbuf, Bhbuf = hid[e8 % 2], Bhid[e8 % 2]
                        for s4 in range(4):
                            for half in range(2):
                                bk, Bbk = bank_ring.next()
                                mm_group(bk[:], [(hbuf[:, hc, s4 * 128:(s4 + 1) * 128], wd[:, hc, half * 512:(half + 1) * 512]) for hc in range(4)],
                                         reads=[Bwd, Bhbuf], bank_buf=Bbk)
                                xs = X[:, s4, half * 512:(half + 1) * 512]
                                S.op("dve", lambda e, bk=bk, xs=xs: e.tensor_tensor(out=xs, in0=bk[:], in1=xs, op=ALU.add),
                                     reads=[Bbk, BX[s4]], writes=[BX[s4]])

                    def pre_c0():
                        for s4 in range(4):
                            norm_T(hb3[s4], Bhb3[s4], hTm, BhTm, s4 * 128)
                        U(0)
                        U(1)

                    def post():
                        for s4 in range(4):
                            rs, Brs = rms_rstd(X[:, s4, :], BX[s4], D, hb3[s4], Bhb3[s4])
                            S.op("dve", lambda e, s4=s4, rs=rs: e.scalar_tensor_tensor(
                                out=X[:, s4, :], in0=X[:, s4, :], scalar=rs, in1=gb4[:, 3, :], op0=ALU.mult, op1=ALU.mult),
                                reads=[BX[s4], Brs, Bgb4], writes=[BX[s4]])
                        out_ops.append(S.op("act", lambda e: e.dma_start(
                            out=y[w * 512:(w + 1) * 512, :].rearrange("(s p) d -> p s d", p=128), in_=X[:]),
                            reads=BX, dma=True))

                    def mid(k):
                        Dn(k - 1)
                        U(k + 1)

                    def last():
                        Dn(6)
                        Dn(7)

                    return [pre_c0] + [(lambda k=k: mid(k)) for k in range(1, 7)] + [last, post]

                Asteps = [h1_steps(i) for i in range(NW)]
                Bparts = [h2_parts(i) for i in range(NW)]
                for k in range(8):
                    Asteps[0][k]()
                for i in range(NW):
                    nxt = Asteps[i + 1] if i + 1 < NW else None
                    if nxt:
                        nxt[0]()
                    Asteps[i][8]()
                    if nxt:
                        nxt[1]()
                    Bparts[i][0]()
                    for k in range(1, 7):
                        if nxt:
                            nxt[k + 1]()
                        Bparts[i][k]()
                    Bparts[i][7]()
                    Bparts[i][8]()
                S.op("sp", lambda e: None, extra_deps=out_ops, nop=True)
                S.op("act", lambda e: None, extra_deps=out_ops, nop=True)
        except _Stop:
            pass
        S.emit(st)
    return nc, S.stats


_CACHE = {}


def kernel(**inputs):
    names = ["x", "mem", "g_mix", "w_in", "conv_w", "g_attn_out", "g_conv_out", "w_out", "g_xattn", "g_mem",
             "w_q_mem", "w_kv_mem", "w_o_mem", "g_mlp", "w_up", "w_down", "g_final"]
    arrs = {k: np.ascontiguousarray(np.asarray(inputs[k], dtype=np.float32)) for k in names}
    if "nc" not in _CACHE:
        _CACHE["nc"] = build_nc()
    nc, stats = _CACHE["nc"]
    n = 8
    in_maps = []
    for b in range(n):
        m = {k: arrs[k] for k in names if k not in ("x", "mem")}
        m["x"] = np.ascontiguousarray(arrs["x"][b])
        m["mem"] = np.ascontiguousarray(arrs["mem"][b])
        in_maps.append(m)
    res = run_bass_kernel_spmd(nc, in_maps, core_ids=list(range(n)))
    return np.stack([np.asarray(r["y"], dtype=np.float32) for r in res.results], axis=0)
```

```python
import numpy as np
from contextlib import ExitStack
import concourse.bass as bass
import concourse.mybir as mybir
from concourse.bass_utils import run_bass_kernel_spmd

F32 = mybir.dt.float32
BF16 = mybir.dt.bfloat16
AF = mybir.ActivationFunctionType
ALU = mybir.AluOpType

S_LEN = 4096
D = 1024
MEM = 256
EPS = 1e-6
NW = 8
NT = 32


class Buf:
    __slots__ = ("w", "r", "name")

    def __init__(self, name=""):
        self.w = None
        self.r = []
        self.name = name


class Op:
    __slots__ = ("eng", "fn", "deps", "has_dep", "sem", "val", "is_dma", "idx")


class Sched:
    ENGS = ("pe", "act", "dve", "pool", "sp")
    N_DMA_SEMS = 40

    def __init__(self, nc):
        self.nc = nc
        self.ops = []
        self.last = {}
        self.dmas = []
        self.stopped = False

    def op(self, eng, fn, reads=(), writes=(), dma=False, extra_deps=(), nop=False, no_barrier=False):
        o = Op()
        if self.stopped:
            return None
        o.eng = eng
        o.fn = fn
        o.is_dma = dma
        o.has_dep = False
        o.sem = None
        o.val = 0
        o.idx = len(self.ops)
        deps = {}
        for b in reads:
            if b.w is not None:
                deps[b.w.idx] = b.w
        for b in writes:
            if b.w is not None:
                deps[b.w.idx] = b.w
            for r in b.r:
                deps[r.idx] = r
        for d in extra_deps:
            if d is not None:
                deps[d.idx] = d
        o.deps = list(deps.values())
        for b in reads:
            b.r.append(o)
        for b in writes:
            b.w = o
            b.r = []
        self.ops.append(o)
        if dma:
            if not no_barrier:
                self.dmas.append(o)
        elif not nop:
            self.last[eng] = o
        return o

    def barrier(self):
        if self.stopped:
            return
        deps = list(self.last.values()) + list(self.dmas)
        self.dmas = []
        for e in self.ENGS:
            self.op(e, lambda eng: None, extra_deps=deps, nop=True)

    def emit(self, stack):
        nc = self.nc

        def skip(d, o):
            return d.eng == "pe" and o.eng == "pe" and not d.is_dma and not o.is_dma

        for o in self.ops:
            for d in o.deps:
                if not skip(d, o):
                    d.has_dep = True
        esem = {e: stack.enter_context(nc.semaphore("sem_" + e)) for e in self.ENGS}
        dsems = [stack.enter_context(nc.semaphore("dsem%d" % i)) for i in range(self.N_DMA_SEMS)]
        dcnt = [0] * self.N_DMA_SEMS
        dlast = [None] * self.N_DMA_SEMS
        cnt = {e: 0 for e in self.ENGS}
        nd = 0
        for o in self.ops:
            if o.is_dma and o.eng == "pool":
                o.sem = stack.enter_context(nc.semaphore("swsem%d" % nd))
                o.val = 16
                nd += 1
            elif o.is_dma:
                i = nd % self.N_DMA_SEMS
                nd += 1
                if dlast[i] is not None:
                    o.deps.append(dlast[i])
                dcnt[i] += 16
                o.sem = dsems[i]
                o.val = dcnt[i]
                dlast[i] = o
            elif o.has_dep:
                cnt[o.eng] += 1
                o.sem = esem[o.eng]
                o.val = cnt[o.eng]
        progs = {e: [] for e in self.ENGS}
        waited = {e: {} for e in self.ENGS}
        nwait = 0
        for o in self.ops:
            w = waited[o.eng]
            waits = {}
            for d in o.deps:
                if skip(d, o):
                    continue
                key = id(d.sem)
                if w.get(key, 0) >= d.val:
                    continue
                w[key] = d.val
                waits[key] = (d.sem, d.val)
            nwait += len(waits)
            progs[o.eng].append((list(waits.values()), o))
        self.stats = dict(n_ops=len(self.ops), n_waits=nwait, counts=dict(cnt), n_dma=nd)

        def replay(eng_name):
            def body(eng):
                for waits, o in progs[eng_name]:
                    for (s, v) in waits:
                        eng.wait_ge(s, v)
                    ins = o.fn(eng)
                    if ins is not None and o.sem is not None:
                        ins.then_inc(o.sem, 16 if o.is_dma else 1)
            return body

        with nc.Block() as block:
            block.tensor(replay("pe"))
            block.scalar(replay("act"))
            block.vector(replay("dve"))
            block.gpsimd(replay("pool"))
            block.sync(replay("sp"))


def ts(start, count, step=1):
    return slice(start, start + (count - 1) * step + 1, step)


class Ring:
    def __init__(self, items):
        self.items = items
        self.i = 0

    def next(self):
        it = self.items[self.i % len(self.items)]
        self.i += 1
        return it


class _Stop(Exception):
    pass


def build_nc(dbg=None):
    dbg = dbg or {}
    nc = bass.Bass("TRN2", target_bir_lowering=False)
    dumps = []

    def din(name, shape):
        return nc.dram_tensor(name, list(shape), F32, kind="ExternalInput").ap()

    x = din("x", (S_LEN, D))
    mem = din("mem", (MEM, D))
    g_mix = din("g_mix", (D,))
    w_in = din("w_in", (D, 3072))
    conv_w = din("conv_w", (3, 512))
    g_attn_out = din("g_attn_out", (512,))
    g_conv_out = din("g_conv_out", (512,))
    w_out = din("w_out", (D, D))
    g_xattn = din("g_xattn", (D,))
    g_mem = din("g_mem", (D,))
    w_q_mem = din("w_q_mem", (D, D))
    w_kv_mem = din("w_kv_mem", (D, 2 * D))
    w_o_mem = din("w_o_mem", (D, D))
    g_mlp = din("g_mlp", (D,))
    w_up = din("w_up", (D, 4 * D))
    w_down = din("w_down", (4 * D, D))
    g_final = din("g_final", (D,))
    y = nc.dram_tensor("y", [S_LEN, D], F32, kind="ExternalOutput").ap()

    def stash(name, shape):
        return nc.dram_tensor(name, list(shape), BF16).ap()

    s_win = stash("s_win", (3, 128, 4096))
    s_wcv = stash("s_wcv", (4, 128, 3072))
    s_wout = stash("s_wout", (2, 128, 4096))
    s_wq = stash("s_wq", (2, 128, 4096))
    s_wkv = stash("s_wkv", (4, 128, 4096))
    s_wo = stash("s_wo", (2, 128, 4096))
    s_wup = stash("s_wup", (8, 128, 4096))
    s_wdn = stash("s_wdn", (8, 128, 4096))

    S = Sched(nc)

    def dump(name, src_ap, shape, dt, reads):
        if S.stopped:
            return
        d = nc.dram_tensor(name, list(shape), dt, kind="ExternalOutput").ap()
        dumps.append(S.op("sp", lambda e: e.dma_start(out=d, in_=src_ap), reads=reads, dma=True))

    def stop():
        S.op("sp", lambda e: None, extra_deps=dumps, nop=True)
        S.stopped = True
    with ExitStack() as st:
        try:
            def sb(name, shape, dt, stack=None):
                return (stack or st).enter_context(nc.sbuf_tensor(name, list(shape), dt))

            pall = [st.enter_context(nc.psum_tensor("pb%d" % i, [128, 512], F32)) for i in range(8)]
            Bpall = [Buf("pb%d" % i) for i in range(8)]
            pall_bf = [p[:].bitcast(BF16) for p in pall]
            pf = pall[2:8]
            Bpf = Bpall[2:8]
            tp_state = {"ring": Ring([(pall_bf[0], Bpall[0]), (pall_bf[1], Bpall[1])])}

            ident = sb("ident", [128, 128], BF16)
            ones_bf = sb("ones_bf", [128, 128], BF16)
            mask4 = sb("mask4", [128, 512], BF16)
            attnT = sb("attnT", [128, 4, S_LEN], BF16)
            ga = sb("ga", [128, 4], F32)
            gc = sb("gc", [128, 4], F32)
            cw = sb("cw", [128, 3, 4], F32)
            rstd_a = sb("rstd_a", [128, NT], F32)
            ssv = sb("ssv", [128, 8], F32)
            rsv = sb("rsv", [128, 8], F32)
            Bconst = Buf("const")
            BattnT = [[Buf("attnT%d_%d" % (p, w)) for w in range(NW)] for p in range(4)]
            Brstd_a = Buf("rstd_a")
            stat_ring = Ring([(i, Buf("ss%d" % i), Buf("rs%d" % i)) for i in range(8)])


            stash_ops = {k: [] for k in ("win0", "win1", "win2", "win", "wout", "wq", "wkv", "wo", "wup", "wdn")}

            def stash_cols(dst_chunk, src, c0, ncol, key):
                kk = src.shape[0] // 128
                S.op("pool", lambda e: e.dma_start(out=dst_chunk.rearrange("p (k c) -> p k c", k=kk),
                                                   in_=src[:, c0:c0 + ncol].rearrange("(k p) c -> p k c", p=128)),
                     writes=[], dma=True, no_barrier=True)
                stash_ops[key].append(S.ops[-1])

            for ci in range(3):
                stash_cols(s_win[ci], w_in, ci * 512, 512, "win%d" % ci)

            def stash_conv():
                for j in range(4):
                    for g in range(3):
                        S.op("pool", lambda e, j=j, g=g: e.dma_start(
                            out=s_wcv[j].rearrange("p (k g c) -> p k g c", k=8, g=3)[:, :, g, :],
                            in_=w_in[:, 1536 + g * 512 + j * 128:1536 + g * 512 + (j + 1) * 128].rearrange("(k p) c -> p k c", p=128)),
                            writes=[], dma=True, no_barrier=True)
                        stash_ops["win"].append(S.ops[-1])

            def stash_rows(dst_chunk, src, r0, nrow, key):
                kk = nrow // 128
                S.op("pool", lambda e: e.dma_start(out=dst_chunk.rearrange("p (k c) -> p k c", k=kk),
                                                   in_=src[r0:r0 + nrow, :].rearrange("(k p) c -> p k c", p=128)),
                     writes=[], dma=True, no_barrier=True)
                stash_ops[key].append(S.ops[-1])

            def mk_consts(e):
                e.memset(ident[:], 0.0)
                e.affine_select(out=ident[:], in_=ident[:], compare_op=ALU.not_equal, fill=1.0,
                                base=0, pattern=[[-1, 128]], channel_multiplier=1)
                e.memset(ones_bf[:], 1.0)
                e.memset(mask4[:], 1.0)
                for blk in range(4):
                    if blk % 2 == 0:
                        e.affine_select(out=mask4[:, blk * 128:(blk + 1) * 128], in_=mask4[:, blk * 128:(blk + 1) * 128],
                                        compare_op=ALU.is_ge, fill=0.0, base=0, pattern=[[-1, 128]], channel_multiplier=1)
                    else:
                        e.affine_select(out=mask4[:, blk * 128:(blk + 1) * 128], in_=mask4[:, blk * 128:(blk + 1) * 128],
                                        compare_op=ALU.is_ge, fill=0.0, base=0, pattern=[[1, 128]], channel_multiplier=-1)
                return None

            S.op("pool", lambda e: e.memset(ident[:], 0.0), writes=[Bconst])
            S.op("pool", lambda e: e.affine_select(out=ident[:], in_=ident[:], compare_op=ALU.not_equal, fill=1.0,
                                                    base=0, pattern=[[-1, 128]], channel_multiplier=1),
                 reads=[Bconst], writes=[Bconst])
            S.op("pool", lambda e: e.memset(ones_bf[:], 1.0), writes=[Bconst])
            S.op("pool", lambda e: e.memset(mask4[:], 1.0), writes=[Bconst])
            for blk in range(4):
                sl = slice(blk * 128, (blk + 1) * 128)
                if blk % 2 == 1:
                    S.op("pool", lambda e, sl=sl: e.affine_select(out=mask4[:, sl], in_=mask4[:, sl], compare_op=ALU.is_ge,
                                                                 fill=0.0, base=0, pattern=[[-1, 128]], channel_multiplier=1),
                         reads=[Bconst], writes=[Bconst])
                else:
                    S.op("pool", lambda e, sl=sl: e.affine_select(out=mask4[:, sl], in_=mask4[:, sl], compare_op=ALU.is_ge,
                                                                 fill=0.0, base=0, pattern=[[1, 128]], channel_multiplier=-1),
                         reads=[Bconst], writes=[Bconst])
            with nc.allow_non_contiguous_dma(reason="tiny gain / conv weight loads"):
                Bgvec = Buf("gvecs")
                Bgcv = Buf("gcv")
                Bcw = [Buf("cw%d" % t) for t in range(3)]
                S.op("act", lambda e: e.dma_start(out=ga[:], in_=g_attn_out.rearrange("(c p) -> p c", p=128), allow_slow_non_contiguous=True),
                     writes=[Bgvec], dma=True)
                S.op("act", lambda e: e.dma_start(out=gc[:], in_=g_conv_out.rearrange("(c p) -> p c", p=128), allow_slow_non_contiguous=True),
                     writes=[Bgcv], dma=True)
                for t in range(3):
                    S.op("act", lambda e, t=t: e.dma_start(out=cw[:, t, :], in_=conv_w[t].rearrange("(c p) -> p c", p=128), allow_slow_non_contiguous=True),
                         writes=[Bcw[t]], dma=True)

            evac_flip = [0]

            def evac(out_ap, in_ap, reads, writes):
                evac_flip[0] ^= 1
                if evac_flip[0]:
                    return S.op("act", lambda e: e.activation(out=out_ap, in_=in_ap, func=AF.Copy), reads=reads, writes=writes)
                return S.op("dve", lambda e: e.tensor_copy(out=out_ap, in_=in_ap), reads=reads, writes=writes)

            def mm_group(out_ap, pairs, reads, bank_buf, first_start=True):
                def fn(e):
                    ins = None
                    n = len(pairs)
                    for i, (l, r) in enumerate(pairs):
                        ins = e.matmul(out_ap, lhsT=l, rhs=r, start=(first_start and i == 0), stop=(i == n - 1),
                                       skip_group_check=True)
                    return ins
                return S.op("pe", fn, reads=reads, writes=[bank_buf])

            def rms_rstd(src_ap, Bsrc, n_feat, scr, Bscr):
                i, Bss, Brs = stat_ring.next()
                S.op("act", lambda e: e.activation(out=scr[:, 0:n_feat], in_=src_ap, func=AF.Square, accum_out=ssv[:, i:i + 1]),
                     reads=[Bsrc], writes=[Bscr, Bss])
                S.op("act", lambda e: e.activation(out=rsv[:, i:i + 1], in_=ssv[:, i:i + 1], func=AF.Ln, scale=1.0 / n_feat, bias=EPS),
                     reads=[Bss], writes=[Brs])
                S.op("act", lambda e: e.activation(out=rsv[:, i:i + 1], in_=rsv[:, i:i + 1], func=AF.Exp, scale=-0.5),
                     reads=[Brs], writes=[Brs])
                return rsv[:, i:i + 1], Brs

            def norm_front(src_ap, Bsrc, gb_ap, Bgb, hb, Bhb):
                rs, Brs = rms_rstd(src_ap, Bsrc, D, hb, Bhb)
                S.op("dve", lambda e: e.scalar_tensor_tensor(out=hb[:], in0=src_ap, scalar=rs, in1=gb_ap,
                                                              op0=ALU.mult, op1=ALU.mult),
                     reads=[Bsrc, Brs, Bgb], writes=[Bhb])

            def norm_to_T(src_ap, Bsrc, gb_ap, Bgb, hb, Bhb, dstT, BdstT, col0):
                norm_front(src_ap, Bsrc, gb_ap, Bgb, hb, Bhb)
                norm_T(hb, Bhb, dstT, BdstT, col0)

            def norm_T(hb, Bhb, dstT, BdstT, col0):
                pt, Bpt = tp_state["ring"].next()

                def tr(e):
                    ins = None
                    for kc in range(8):
                        ins = e.transpose(out=pt[:, kc * 128:(kc + 1) * 128], in_=hb[:, kc * 128:(kc + 1) * 128], identity=ident[:])
                    return ins
                S.op("pe", tr, reads=[Bhb, Bconst], writes=[Bpt])
                evac(dstT[:, :, col0:col0 + 128], pt[:].rearrange("p (k t) -> p k t", k=8), reads=[Bpt], writes=[BdstT])

            with ExitStack() as p1:
                KTA = sb("KTA", [128, 4, S_LEN], BF16, p1)
                VTA = sb("VTA", [128, 4, S_LEN], BF16, p1)
                BK = [[Buf("K%d_%d" % (p, w)) for w in range(NW)] for p in range(4)]
                BV = [[Buf("V%d_%d" % (p, w)) for w in range(NW)] for p in range(4)]
                BQ = BattnT

                with ExitStack() as p1a:
                    gbm = sb("gbm", [128, D], F32, p1a)
                    Bgbm = Buf("gbm")
                    S.op("sp", lambda e: e.dma_start(out=gbm[:], in_=g_mix.partition_broadcast(128)), writes=[Bgbm], dma=True)
                    win_sb = sb("win_sb", [128, 8, 1536], BF16, p1a)
                    Bwin = [Buf("win%d" % i) for i in range(3)]
                    xw = [sb("xw%d" % i, [128, 4, D], F32, p1a) for i in range(2)]
                    Bxw = [[Buf("xw%d_%d" % (i, s4)) for s4 in range(4)] for i in range(2)]
                    hbs = [[sb("hbs%d_%d" % (i, s4), [128, D], BF16, p1a) for s4 in range(4)] for i in range(2)]
                    Bhbs = [[Buf("hbs%d_%d" % (i, s4)) for s4 in range(4)] for i in range(2)]
                    hTw1 = [sb("hTw1_%d" % i, [128, 8, 512], BF16, p1a) for i in range(2)]
                    BhTw1 = [Buf("hTw1_%d" % i) for i in range(2)]
                    proj_ring = Ring(list(zip(pf, Bpf)))

                    def front1(w):
                        b = w % 2
                        S.op("sp", lambda e: e.dma_start(
                            out=xw[b][:], in_=x[w * 512:(w + 1) * 512, :].rearrange("(s p) d -> p s d", p=128)),
                            writes=Bxw[b], dma=True)
                        for s4 in range(4):
                            norm_front(xw[b][:, s4, :], Bxw[b][s4], gbm[:], Bgbm, hbs[b][s4], Bhbs[b][s4])

                    def T1(w):
                        b = w % 2
                        for s4 in range(4):
                            norm_T(hbs[b][s4], Bhbs[b][s4], hTw1[b], BhTw1[b], s4 * 128)

                    def proj1(w):
                        b = w % 2
                        for ci, (dst, Bdst) in enumerate(((attnT, BQ), (KTA, BK), (VTA, BV))):
                            for p in range(4):
                                c0 = ci * 512 + p * 128
                                bk, Bbk = proj_ring.next()
                                mm_group(bk[:], [(win_sb[:, kc, c0:c0 + 128], hTw1[b][:, kc, :]) for kc in range(8)],
                                         reads=[Bwin[ci], BhTw1[b]], bank_buf=Bbk)
                                evac(dst[:, p, w * 512:(w + 1) * 512], bk[:], reads=[Bbk], writes=[Bdst[p][w]])

                    front1(0)
                    front1(1)
                    for ci in range(3):
                        S.op("sp", lambda e, ci=ci: e.dma_start(
                            out=win_sb[:, :, ci * 512:(ci + 1) * 512],
                            in_=s_win[ci].rearrange("p (k c) -> p k c", k=8)),
                            writes=[Bwin[ci]], dma=True, extra_deps=stash_ops["win%d" % ci])
                    T1(0)
                    for w in range(NW):
                        if w + 2 < NW:
                            front1(w + 2)
                        if w + 1 < NW:
                            T1(w + 1)
                        proj1(w)
                    S.barrier()

                with ExitStack() as p1c:
                    Vp = {d: sb("Vp%d" % d, [128, NT, 192], BF16, p1c) for d in (1, 4, 16)}
                    BVp = {d: [Buf("Vp%d_%d" % (d, i)) for i in range(NT)] for d in (1, 4, 16)}
                    QT4 = sb("QT4", [128, NW, 4, 128], BF16, p1c)
                    KT4 = sb("KT4", [128, NW, 4, 128], BF16, p1c)
                    QT16 = sb("QT16", [128, 2, 16, 128], BF16, p1c)
                    KT16 = sb("KT16", [128, 2, 16, 128], BF16, p1c)
                    BQT4, BKT4, BQT16, BKT16 = Buf("QT4"), Buf("KT4"), Buf("QT16"), Buf("KT16")
                    P1 = [sb("P1_%d" % i, [128, 4, 2, 128], BF16, p1c) for i in range(3)]
                    P4 = [sb("P4_%d" % i, [128, 4, 2, 128], BF16, p1c) for i in range(3)]
                    BP1 = [[Buf("P1_%d_%d" % (i, k)) for k in range(2)] for i in range(3)]
                    BP4 = [[Buf("P4_%d_%d" % (i, k)) for k in range(2)] for i in range(3)]
                    P16b = sb("P16b", [128, 16, 128], BF16, p1c)
                    BP16b = [Buf("P16b_%d" % k) for k in range(8)]
                    tmpn = [sb("tmpn%d" % i, [128, 512], F32, p1c) for i in range(2)]
                    Btmpn = [Buf("tmpn%d" % i) for i in range(2)]
                    recn = [sb("recn%d" % i, [128, 512], F32, p1c) for i in range(2)]
                    Brecn = [Buf("recn%d" % i) for i in range(2)]
                    sqp = sb("sqp", [128, S_LEN], BF16, p1c)
                    Bsqp = [Buf("sqp%d" % w) for w in range(NW)]
                    tp_state["ring"] = Ring([(pall_bf[0], Bpall[0]), (pall_bf[7], Bpall[7])])
                    S_ring = Ring([(pall[1], Bpall[1]), (pall[2], Bpall[2]), (pall[3], Bpall[3]), (pall[7], Bpall[7])])
                    O_ring = Ring([(pall[4], Bpall[4]), (pall[5], Bpall[5])])
                    ssq_bank, Bssq = pall[6], Bpall[6]

                    for d in (1, 4, 16):
                        S.op("pool", lambda e, d=d: e.memset(Vp[d][:, :, 64:128], 1.0), writes=BVp[d])

                    ssq_started = [False]
                    mflip = [0]

                    def mask_mul(dst, msk, Bd):
                        S.op("dve", lambda e: e.tensor_tensor(out=dst, in0=dst, in1=msk, op=ALU.mult),
                             reads=[Bd, Bconst], writes=[Bd])

                    def do_pair(pair):
                        VTp = VTA[:, pair, :]
                        P16a = VTp.rearrange("p (r a b) -> p r a b", r=16, a=2)
                        BP16a = BV[pair]
                        for d in (1, 4, 16):
                            for t0 in range(0, NT, 8):
                                pt, Bpt = tp_state["ring"].next()

                                def trv(e, d=d, t0=t0, pt=pt):
                                    ins = None
                                    for j in range(8):
                                        t = t0 + j
                                        if d == 1:
                                            src = VTp[:, t * 128:(t + 1) * 128]
                                        elif d == 4:
                                            src = VTp[:, ts(512 * (t // 4) + (t % 4), 128, 4)]
                                        else:
                                            src = VTp[:, ts(2048 * (t // 16) + (t % 16), 128, 16)]
                                        ins = e.transpose(out=pt[:, j * 128:(j + 1) * 128], in_=src, identity=ident[:])
                                    return ins
                                S.op("pe", trv, reads=BV[pair] + [Bconst], writes=[Bpt])
                                evac(Vp[d][:, t0:t0 + 8, :].rearrange("p t (b e) -> p t b e", b=3)[:, :, 0:3:2, :],
                                     pt[:].rearrange("p (t b e) -> p t b e", t=8, b=2),
                                     reads=[Bpt], writes=BVp[d][t0:t0 + 8])
                        Qp, Kp = attnT[:, pair, :], KTA[:, pair, :]
                        S.op("dve", lambda e, Qp=Qp: e.tensor_copy(out=QT16[:], in_=Qp.rearrange("p (n i r) -> p n r i", r=16, i=128)),
                             reads=BQ[pair], writes=[BQT16])
                        S.op("act", lambda e, Kp=Kp: e.activation(out=KT16[:, 0:1, :, :], in_=Kp[:, 0:2048].rearrange("p (n i r) -> p n r i", r=16, i=128), func=AF.Copy),
                             reads=BK[pair], writes=[BKT16])
                        S.op("dve", lambda e, Kp=Kp: e.tensor_copy(out=KT16[:, 1:2, :, :], in_=Kp[:, 2048:4096].rearrange("p (n i r) -> p n r i", r=16, i=128)),
                             reads=BK[pair], writes=[BKT16])
                        S.op("dve", lambda e, Qp=Qp: e.tensor_copy(out=QT4[:], in_=Qp.rearrange("p (n i r) -> p n r i", r=4, i=128)),
                             reads=BQ[pair], writes=[BQT4])
                        S.op("act", lambda e, Kp=Kp: e.activation(out=KT4[:], in_=Kp.rearrange("p (n i r) -> p n r i", r=4, i=128), func=AF.Copy),
                             reads=BK[pair], writes=[BKT4])

                        if dbg.get("stop") == "proj" and pair == 0:
                            dump("d_QT", attnT[:, 0, :], [128, S_LEN], BF16, BQ[0])
                            dump("d_KT", KTA[:, 0, :], [128, S_LEN], BF16, BK[0])
                            dump("d_QT16", QT16[:], [128, 2, 16, 128], BF16, [BQT16])
                            dump("d_KT4", KT4[:], [128, NW, 4, 128], BF16, [BKT4])
                            for d in (1, 4, 16):
                                dump("d_Vp%d" % d, Vp[d][:], [128, NT, 192], BF16, BVp[d])
                            stop()

                        def score_tiles(hh, tiles):
                            for i in range(0, len(tiles), 2):
                                chunk = tiles[i:i + 2]
                                sbank, Bsb = S_ring.next()
                                mms = []
                                rds = []
                                for j, (kT, q, hn, Pd, Bd, rd, _pd) in enumerate(chunk):
                                    n = 256 if hn else 128
                                    mms.append((sbank[:, j * 256:j * 256 + n], kT, q))
                                    rds += rd

                                def fn(e, mms=mms):
                                    ins = None
                                    for k, (o, l, r) in enumerate(mms):
                                        ins = e.matmul(o, lhsT=l, rhs=r, start=(k == 0), stop=(k == len(mms) - 1),
                                                       skip_group_check=True)
                                    return ins
                                S.op("pe", fn, reads=rds, writes=[Bsb])
                                if len(chunk) == 2 and chunk[0][2] and chunk[1][2] and chunk[0][4] is chunk[1][4] and dbg.get("fuse", 1):
                                    Pd0 = chunk[0][3]
                                    Bd = chunk[0][4]
                                    dst = chunk[0][6]
                                    S.op("act", lambda e, dst=dst, sbank=sbank: e.activation(out=dst, in_=sbank[:], func=AF.Exp, scale=0.125),
                                         reads=[Bsb], writes=[Bd])
                                    mask_mul(dst, mask4[:], Bd)
                                else:
                                    for j, (kT, q, hn, Pd, Bd, rd, _pd) in enumerate(chunk[:2]):
                                        n = 256 if hn else 128
                                        src = sbank[:, j * 256:j * 256 + n]
                                        dst = Pd.rearrange("p a b -> p (a b)") if hn else Pd
                                        S.op("act", lambda e, dst=dst, src=src: e.activation(out=dst, in_=src, func=AF.Exp, scale=0.125),
                                             reads=[Bsb], writes=[Bd])
                                        mask_mul(dst, mask4[:, 0:n], Bd)

                        def S16(hh, n2):
                            hp = slice(64 * hh, 64 * hh + 64)
                            tiles = []
                            for r in range(16):
                                kT = KT16[hp, n2, r, :]
                                if n2 == 0:
                                    q = QT16[hp, 0:2, r, :]
                                    Pd = P16a[:, r, :, :]
                                    Bd = BP16a[r // 2]
                                    pairdst = P16a[:, r - 1:r + 1, :, :].rearrange("p r a b -> p (r a b)") if r % 2 == 1 else None
                                    tiles.append([kT, q, True, Pd, Bd, [BKT16, BQT16], pairdst])
                                else:
                                    q = QT16[hp, 1, r, :]
                                    tiles.append([kT, q, False, P16b[:, r, :], BP16b[r // 2], [BKT16, BQT16], None])
                            for i in range(0, 16, 2):
                                tiles[i][6] = tiles[i + 1][6]
                            score_tiles(hh, [tuple(t) for t in tiles])

                        def S1(hh, w):
                            hp = slice(64 * hh, 64 * hh + 64)
                            b = w % 3
                            tiles = []
                            for g in range(4):
                                t = 4 * w + g
                                hn = t + 1 < NT
                                kT = KTA[hp, pair, 128 * t:128 * t + 128]
                                q = attnT[hp, pair, 128 * t:128 * t + (256 if hn else 128)]
                                rd = [BK[pair][w], BQ[pair][w]] + ([BQ[pair][w + 1]] if (g == 3 and hn) else [])
                                Pd = P1[b][:, g, :, :] if hn else P1[b][:, g, 0, :]
                                pairdst = P1[b][:, g - 1:g + 1, :, :].rearrange("p r a b -> p (r a b)") if g % 2 == 1 else None
                                tiles.append((kT, q, hn, Pd, BP1[b][g // 2], rd, pairdst))
                            tiles = [(t[0], t[1], t[2], t[3], t[4], t[5], tiles[(i // 2) * 2 + 1][6]) for i, t in enumerate(tiles)]
                            score_tiles(hh, tiles)

                        def S4(hh, w):
                            hp = slice(64 * hh, 64 * hh + 64)
                            b = w % 3
                            hn = w + 1 < NW
                            tiles = []
                            for r in range(4):
                                kT = KT4[hp, w, r, :]
                                q = QT4[hp, w:w + 2, r, :] if hn else QT4[hp, w, r, :]
                                Pd = P4[b][:, r, :, :] if hn else P4[b][:, r, 0, :]
                                pairdst = P4[b][:, r - 1:r + 1, :, :].rearrange("p r a b -> p (r a b)") if r % 2 == 1 else None
                                tiles.append((kT, q, hn, Pd, BP4[b][r // 2], [BKT4, BQT4], pairdst))
                            tiles = [(t[0], t[1], t[2], t[3], t[4], t[5], tiles[(i // 2) * 2 + 1][6]) for i, t in enumerate(tiles)]
                            score_tiles(hh, tiles)

                        def PV_norm(hh, w):
                            vsl = slice(64 * hh, 64 * hh + 128)
                            nump = slice(64 * hh, 64 * hh + 64)
                            denp = slice(64 * (1 - hh), 64 * (1 - hh) + 64)
                            n2, ww = w // 4, w % 4
                            b, pb = w % 3, (w - 1) % 3
                            ob, Bob = O_ring.next()
                            pv = []
                            for g in range(4):
                                t = 4 * w + g
                                pv.append((ob[:, g * 128:(g + 1) * 128], Vp[1][:, t, vsl], P1[b][:, g, 0, :]))
                                if t >= 1:
                                    prevP = P1[b][:, g - 1, 1, :] if g >= 1 else P1[pb][:, 3, 1, :]
                                    pv.append((ob[:, g * 128:(g + 1) * 128], Vp[1][:, t - 1, vsl], prevP))
                            for r in range(4):
                                pv.append((ob[:, ts(r, 128, 4)], Vp[4][:, 4 * w + r, vsl], P4[b][:, r, 0, :]))
                                if w >= 1:
                                    pv.append((ob[:, ts(r, 128, 4)], Vp[4][:, 4 * (w - 1) + r, vsl], P4[pb][:, r, 1, :]))
                            for r in range(16):
                                csl = slice(32 * ww, 32 * ww + 32)
                                if n2 == 0:
                                    pv.append((ob[:, ts(r, 32, 16)], Vp[16][:, r, vsl], P16a[:, r, 0, csl]))
                                else:
                                    pv.append((ob[:, ts(r, 32, 16)], Vp[16][:, 16 + r, vsl], P16b[:, r, csl]))
                                    pv.append((ob[:, ts(r, 32, 16)], Vp[16][:, r, vsl], P16a[:, r, 1, csl]))

                            def pvfn(e, pv=pv):
                                ins = None
                                for k, (o, l, r) in enumerate(pv):
                                    ins = e.matmul(o, lhsT=l, rhs=r, start=(k == 0), stop=(k == len(pv) - 1),
                                                   skip_group_check=True)
                                return ins
                            S.op("pe", pvfn, reads=BP1[b] + BP1[pb] + BP4[b] + BP4[pb] + BP16a + (BP16b if n2 == 1 else []) + BVp[1] + BVp[4] + BVp[16],
                                 writes=[Bob])
                            nb = (hh * NW + w) % 2
                            S.op("act", lambda e: e.activation(out=recn[nb][denp, :], in_=ob[denp, :], func=AF.Ln),
                                 reads=[Bob], writes=[Brecn[nb]])
                            S.op("act", lambda e: e.activation(out=recn[nb][denp, :], in_=recn[nb][denp, :], func=AF.Exp, scale=-1.0),
                                 reads=[Brecn[nb]], writes=[Brecn[nb]])
                            S.op("dve", lambda e: e.tensor_tensor(out=tmpn[nb][nump, :], in0=ob[nump, :], in1=recn[nb][denp, :], op=ALU.mult),
                                 reads=[Bob, Brecn[nb]], writes=[Btmpn[nb]])
                            S.op("pool", lambda e: e.tensor_tensor(out=sqp[nump, w * 512:(w + 1) * 512], in0=tmpn[nb][nump, :],
                                                                   in1=tmpn[nb][nump, :], op=ALU.mult),
                                 reads=[Btmpn[nb]], writes=[Bsqp[w]])
                            S.op("pool", lambda e: e.tensor_scalar(
                                out=attnT[nump, pair, w * 512:(w + 1) * 512], in0=tmpn[nb][nump, :],
                                scalar1=ga[nump, pair:pair + 1], scalar2=1.0, op0=ALU.mult, op1=ALU.mult),
                                reads=[Btmpn[nb], Bgvec], writes=[BattnT[pair][w]])

                        for hh in range(2):
                            S16(hh, 0)
                            S1(hh, 0)
                            S4(hh, 0)
                            for w in range(NW):
                                if w + 1 < NW:
                                    if w + 1 == 4:
                                        S16(hh, 1)
                                    S1(hh, w + 1)
                                    S4(hh, w + 1)
                                PV_norm(hh, w)
                        for w in range(NW):
                            def ssfn(e, w=w, first=not ssq_started[0]):
                                ins = None
                                for s4 in range(4):
                                    tt = 4 * w + s4
                                    ins = e.matmul(ssq_bank[:, tt:tt + 1], lhsT=sqp[:, tt * 128:(tt + 1) * 128], rhs=ones_bf[:, 0:1],
                                                   start=(first and s4 == 0), stop=True, skip_group_check=True)
                                return ins
                            S.op("pe", ssfn, reads=[Bsqp[w], Bconst], writes=[Bssq])
                            ssq_started[0] = True
                        if dbg.get("stop") == "attn0" and pair == 0:
                            dump("d_attnT", attnT[:, 0, :], [128, S_LEN], BF16, BattnT[0])
                            stop()
                        if not dbg.get("stash2", True):
                            pass
                        elif pair == 0:
                            stash_conv()
                            for c in range(4):
                                stash_cols(s_wkv[c], w_kv_mem, c * 512, 512, "wkv")
                            for c in range(2):
                                stash_cols(s_wout[c], w_out, c * 512, 512, "wout")
                            for c in range(2):
                                stash_cols(s_wq[c], w_q_mem, c * 512, 512, "wq")
                            for c in range(2):
                                stash_cols(s_wo[c], w_o_mem, c * 512, 512, "wo")
                        elif pair == 1:
                            for c in range(8):
                                stash_cols(s_wup[c], w_up, c * 512, 512, "wup")
                        elif pair == 2:
                            for c in range(8):
                                stash_rows(s_wdn[c], w_down, c * 512, 512, "wdn")

                    for pair_i in range(dbg.get("pairs", 4)):
                        do_pair(pair_i)

                    S.op("act", lambda e: e.activation(out=rstd_a[:], in_=ssq_bank[:, 0:NT], func=AF.Ln, scale=1.0 / 512, bias=EPS),
                         reads=[Bssq], writes=[Brstd_a])
                    S.op("act", lambda e: e.activation(out=rstd_a[:], in_=rstd_a[:], func=AF.Exp, scale=-0.5),
                         reads=[Brstd_a], writes=[Brstd_a])
                    S.barrier()
                    tp_state["ring"] = Ring([(pall_bf[0], Bpall[0]), (pall_bf[1], Bpall[1])])
                    if dbg.get("stop") == "1c":
                        dump("d_attnT", attnT[:], [128, 4, S_LEN], BF16, sum(BattnT, []))
                        dump("d_rstd_a", rstd_a[:], [128, NT], F32, [Brstd_a])
                        stop()

            with ExitStack() as p2:
                gb4 = sb("gb4", [128, 4, D], F32, p2)
                Bgb4 = Buf("gb4")
                for i, gvec in enumerate((g_mix, g_xattn, g_mlp, g_final)):
                    S.op("sp", lambda e, i=i, gvec=gvec: e.dma_start(out=gb4[:, i, :], in_=gvec.partition_broadcast(128)),
                         writes=[Bgb4], dma=True)
                memKT = sb("memKT", [128, 8, MEM], BF16, p2)
                memV = sb("memV", [128, 2, D], BF16, p2)
                BmemKT, BmemV = Buf("memKT"), Buf("memV")
                uw = sb("uw", [128, 514], F32, p2)
                Buw = Buf("uw")
                ucar = sb("ucar", [128, 4, 2], F32, p2)
                Bucar = [Buf("ucar%d" % j) for j in range(4)]
                hTm = sb("hTm", [128, 8, 512], BF16, p2)
                BhTm = Buf("hTm")
                xr = [sb("xr%d" % i, [128, 4, D], F32, p2) for i in range(2)]
                Bxr = [[Buf("xr%d_%d" % (i, s4)) for s4 in range(4)] for i in range(2)]
                hTw = sb("hTw", [128, 8, 512], BF16, p2)
                BhTw = Buf("hTw")
                hb2 = [sb("hb2_%d" % i, [128, D], BF16, p2) for i in range(4)]
                Bhb2 = [Buf("hb2_%d" % i) for i in range(4)]
                xcs = sb("xcs", [128, 512], F32, p2)
                acc = sb("acc", [128, 512], F32, p2)
                ycv = sb("ycv", [128, 512], F32, p2)
                Bxcs, Bacc, Bycv = Buf("xcs"), Buf("acc"), Buf("ycv")
                sqc = sb("sqc", [128, 4, 512], BF16, p2)
                convT = sb("convT", [128, 4, 512], BF16, p2)
                Bsqc, BconvT = Buf("sqc"), Buf("convT")
                rstd_c = sb("rstd_c", [128, 4], F32, p2)
                Brstd_c = Buf("rstd_c")
                NRING = 6
                ring_t = [sb("wring%d" % i, [128, 4096], BF16, p2) for i in range(NRING)]
                wring = Ring([(ring_t[i], Buf("wring%d" % i)) for i in range(NRING)])
                qT = sb("qT", [128, 8, 512], BF16, p2)
                BqT = Buf("qT")
                PX = [sb("PX%d" % i, [128, 2, 512], BF16, p2) for i in range(2)]
                BPX = [Buf("PX%d" % i) for i in range(2)]
                recx = [sb("recx%d" % i, [128, 512], F32, p2) for i in range(1)] * 2
                Brecx = [Buf("recx%d" % i) for i in range(1)] * 2
                oT = hTw
                BoT = BhTw
                rl = [sb("rl%d" % i, [128, 512], F32, p2) for i in range(2)]
                Brl = [Buf("rl%d" % i) for i in range(2)]
                hid = [sb("hid%d" % i, [128, 4, 512], BF16, p2) for i in range(2)]
                Bhid = [Buf("hid%d" % i) for i in range(2)]
                bank_ring = Ring(list(zip(pf, Bpf)))

                def wload(src_ap, a, b, dep_key):
                    t, Bt = wring.next()
                    view = t[:, 0:a * b].rearrange("p (a b) -> p a b", a=a)
                    S.op("sp", lambda e: e.dma_start(out=view, in_=src_ap.rearrange("p (a b) -> p a b", a=a)), writes=[Bt], dma=True,
                         extra_deps=stash_ops[dep_key])
                    return view, Bt

                S.op("pool", lambda e: e.memset(ucar[:], 0.0), writes=Bucar)

                with ExitStack() as pm:
                    mt_ = [xr[0][:, 0, :], xr[0][:, 1, :]]
                    Bmt = [Bxr[0][0], Bxr[0][1]]
                    gbmem = xr[0][:, 2, :]
                    Bgbmem = Bxr[0][2]
                    memhT = hTw[:, :, 0:MEM]
                    BmemhT = BhTw
                    S.op("sp", lambda e: e.dma_start(out=gbmem, in_=g_mem.partition_broadcast(128)), writes=[Bgbmem], dma=True)
                    for i in range(2):
                        S.op("sp", lambda e, i=i: e.dma_start(out=mt_[i], in_=mem[i * 128:(i + 1) * 128, :]), writes=[Bmt[i]], dma=True)
                        norm_to_T(mt_[i], Bmt[i], gbmem, Bgbmem, hb2[i], Bhb2[i], memhT, BmemhT, i * 128)
                    for half in range(2):
                        wv, Bw = wload(s_wkv[half], 8, 512, "wkv")
                        for cc in range(4):
                            c = half * 4 + cc
                            bk, Bbk = bank_ring.next()
                            mm_group(bk[:, 0:MEM], [(wv[:, kc, cc * 128:(cc + 1) * 128], memhT[:, kc, :]) for kc in range(8)],
                                     reads=[Bw, BmemhT], bank_buf=Bbk)
                            evac(memKT[:, c, :], bk[:, 0:MEM], reads=[Bbk], writes=[BmemKT])
                    for half in range(2):
                        wv, Bw = wload(s_wkv[2 + half], 8, 512, "wkv")
                        for mt in range(2):
                            bk, Bbk = bank_ring.next()
                            mm_group(bk[:], [(memhT[:, kc, mt * 128:(mt + 1) * 128], wv[:, kc, :]) for kc in range(8)],
                                     reads=[Bw, BmemhT], bank_buf=Bbk)
                            evac(memV[:, mt, half * 512:(half + 1) * 512], bk[:], reads=[Bbk], writes=[BmemV])

                out_ops = []

                def h1_steps(w):
                    xb = w % 2
                    X, BX = xr[xb], Bxr[xb]

                    def s_load():
                        S.op("sp", lambda e: e.dma_start(
                            out=X[:], in_=x[w * 512:(w + 1) * 512, :].rearrange("(s p) d -> p s d", p=128)),
                            writes=BX, dma=True)
                        for s4 in range(4):
                            norm_front(X[:, s4, :], BX[s4], gb4[:, 0, :], Bgb4, hb2[s4], Bhb2[s4])

                    def s_T0conv0():
                        for s4 in range(4):
                            norm_T(hb2[s4], Bhb2[s4], hTw, BhTw, s4 * 128)
                        s_conv(0)

                    def s_conv(j):
                        t, Bt = wring.next()
                        wv = t[:, 0:8 * 384].rearrange("p (a b) -> p a b", a=8)
                        S.op("sp", lambda e: e.dma_start(out=wv, in_=s_wcv[j].rearrange("p (a b) -> p a b", a=8)),
                             writes=[Bt], dma=True, extra_deps=stash_ops["win"])
                        banks = [bank_ring.next() for _ in range(3)]
                        for ci in range(3):
                            mm_group(banks[ci][0][:], [(wv[:, kc, ci * 128:(ci + 1) * 128], hTw[:, kc, :]) for kc in range(8)],
                                     reads=[Bt, BhTw], bank_buf=banks[ci][1])
                        (bgp, Bbg), (cgp, Bcg), (xcp, Bxc) = banks
                        S.op("pool", lambda e: e.tensor_copy(out=uw[:, 0:2], in_=ucar[:, j, :]), reads=[Bucar[j]], writes=[Buw])
                        S.op("act", lambda e: e.activation(out=xcs[:], in_=xcp[:], func=AF.Copy), reads=[Bxc], writes=[Bxcs])
                        S.op("act", lambda e: e.activation(out=uw[:, 2:514], in_=cgp[:], func=AF.Copy), reads=[Bcg], writes=[Buw])
                        S.op("act", lambda e: e.activation(out=ycv[:], in_=bgp[:], func=AF.Copy), reads=[Bbg], writes=[Bycv])
                        S.op("pool", lambda e: e.tensor_tensor(out=uw[:, 2:514], in0=uw[:, 2:514], in1=xcs[:], op=ALU.mult),
                             reads=[Buw, Bxcs], writes=[Buw])
                        S.op("pool", lambda e: e.tensor_scalar(out=acc[:], in0=uw[:, 2:514], scalar1=cw[:, 2, j:j + 1], scalar2=1.0,
                                                               op0=ALU.mult, op1=ALU.mult),
                             reads=[Buw] + Bcw, writes=[Bacc])
                        for tap, sl in ((1, slice(1, 513)), (0, slice(0, 512))):
                            S.op("pool", lambda e, tap=tap, sl=sl: e.tensor_scalar(out=xcs[:], in0=uw[:, sl], scalar1=cw[:, tap, j:j + 1], scalar2=1.0,
                                                                                   op0=ALU.mult, op1=ALU.mult),
                                 reads=[Buw] + Bcw, writes=[Bxcs])
                            S.op("pool", lambda e: e.tensor_tensor(out=acc[:], in0=acc[:], in1=xcs[:], op=ALU.add),
                                 reads=[Bacc, Bxcs], writes=[Bacc])
                        S.op("pool", lambda e: e.tensor_tensor(out=ycv[:], in0=ycv[:], in1=acc[:], op=ALU.mult),
                             reads=[Bycv, Bacc], writes=[Bycv])
                        S.op("act", lambda e: e.activation(out=sqc[:, j, :], in_=ycv[:], func=AF.Square), reads=[Bycv], writes=[Bsqc])
                        S.op("pool", lambda e: e.tensor_scalar(out=convT[:, j, :], in0=ycv[:], scalar1=gc[:, j:j + 1], scalar2=1.0,
                                                               op0=ALU.mult, op1=ALU.mult),
                             reads=[Bycv, Bgcv], writes=[BconvT])
                        S.op("pool", lambda e: e.tensor_copy(out=ucar[:, j, :], in_=uw[:, 512:514]), reads=[Buw], writes=[Bucar[j]])

                    def st_wout():
                        bk, Bbk = bank_ring.next()

                        def ssc(e):
                            ins = None
                            k = 0
                            for s4 in range(4):
                                for j in range(4):
                                    ins = e.matmul(bk[:, s4:s4 + 1], lhsT=sqc[:, j, s4 * 128:(s4 + 1) * 128], rhs=ones_bf[:, 0:1],
                                                   start=(k == 0), stop=(k == 15), skip_group_check=True)
                                    k += 1
                            return ins
                        S.op("pe", ssc, reads=[Bsqc, Bconst], writes=[Bbk])
                        S.op("act", lambda e: e.activation(out=rstd_c[:], in_=bk[:, 0:4], func=AF.Ln, scale=1.0 / 512, bias=EPS),
                             reads=[Bbk], writes=[Brstd_c])
                        S.op("act", lambda e: e.activation(out=rstd_c[:], in_=rstd_c[:], func=AF.Exp, scale=-0.5),
                             reads=[Brstd_c], writes=[Brstd_c])
                        wvs = [wload(s_wout[half], 8, 512, "wout")
                               for half in range(2)]
                        for s4 in range(4):
                            for half in range(2):
                                wv, Bw = wvs[half]
                                tsl = slice(w * 512 + s4 * 128, w * 512 + (s4 + 1) * 128)
                                pa, Bpa = bank_ring.next()
                                pc, Bpc = bank_ring.next()
                                mm_group(pa[:], [(attnT[:, kc, tsl], wv[:, kc, :]) for kc in range(4)],
                                         reads=[Bw] + [BattnT[p][w] for p in range(4)], bank_buf=Bpa)
                                mm_group(pc[:], [(convT[:, kc, s4 * 128:(s4 + 1) * 128], wv[:, 4 + kc, :]) for kc in range(4)],
                                         reads=[Bw, BconvT], bank_buf=Bpc)
                                xs = X[:, s4, half * 512:(half + 1) * 512]
                                tt = 4 * w + s4
                                S.op("dve", lambda e, pa=pa, xs=xs, tt=tt: e.scalar_tensor_tensor(
                                    out=xs, in0=pa[:], scalar=rstd_a[:, tt:tt + 1], in1=xs, op0=ALU.mult, op1=ALU.add),
                                    reads=[Bpa, Brstd_a, BX[s4]], writes=[BX[s4]])
                                S.op("dve", lambda e, pc=pc, xs=xs, s4=s4: e.scalar_tensor_tensor(
                                    out=xs, in0=pc[:], scalar=rstd_c[:, s4:s4 + 1], in1=xs, op0=ALU.mult, op1=ALU.add),
                                    reads=[Bpc, Brstd_c, BX[s4]], writes=[BX[s4]])
                            norm_front(X[:, s4, :], BX[s4], gb4[:, 1, :], Bgb4, hb2[s4], Bhb2[s4])

                    def s_q():
                        for s4 in range(4):
                            norm_T(hb2[s4], Bhb2[s4], hTw, BhTw, s4 * 128)
                        for half in range(2):
                            wv, Bw = wload(s_wq[half], 8, 512, "wq")
                            for cc in range(4):
                                c = half * 4 + cc
                                bk, Bbk = bank_ring.next()
                                mm_group(bk[:], [(wv[:, kc, cc * 128:(cc + 1) * 128], hTw[:, kc, :]) for kc in range(8)],
                                         reads=[Bw, BhTw], bank_buf=Bbk)
                                evac(qT[:, c, :], bk[:], reads=[Bbk], writes=[BqT])

                    def s_xattn():
                        for h in range(4):
                            pb = h % 2
                            for mt in range(2):
                                bk, Bbk = bank_ring.next()
                                mm_group(bk[:], [(memKT[:, 2 * h + cc, mt * 128:(mt + 1) * 128], qT[:, 2 * h + cc, :]) for cc in range(2)],
                                         reads=[BmemKT, BqT], bank_buf=Bbk)
                                S.op("act", lambda e, bk=bk, pb=pb, mt=mt: e.activation(out=PX[pb][:, mt, :], in_=bk[:], func=AF.Exp, scale=1.0 / 16),
                                     reads=[Bbk], writes=[BPX[pb]])
                            bk, Bbk = bank_ring.next()
                            mm_group(bk[:], [(ones_bf[:], PX[pb][:, mt, :]) for mt in range(2)], reads=[Bconst, BPX[pb]], bank_buf=Bbk)
                            S.op("act", lambda e, bk=bk, pb=pb: e.activation(out=recx[pb][:], in_=bk[:], func=AF.Ln), reads=[Bbk], writes=[Brecx[pb]])
                            S.op("act", lambda e, pb=pb: e.activation(out=recx[pb][:], in_=recx[pb][:], func=AF.Exp, scale=-1.0),
                                 reads=[Brecx[pb]], writes=[Brecx[pb]])
                            for cc in range(2):
                                c = 2 * h + cc
                                bk, Bbk = bank_ring.next()
                                mm_group(bk[:], [(memV[:, mt, c * 128:(c + 1) * 128], PX[pb][:, mt, :]) for mt in range(2)],
                                         reads=[BmemV, BPX[pb]], bank_buf=Bbk)
                                S.op("dve", lambda e, bk=bk, pb=pb, c=c: e.tensor_tensor(out=oT[:, c, :], in0=bk[:], in1=recx[pb][:], op=ALU.mult),
                                     reads=[Bbk, Brecx[pb]], writes=[BoT])

                    def st_wo():
                        wvs = [wload(s_wo[half], 8, 512, "wo")
                               for half in range(2)]
                        for s4 in range(4):
                            for half in range(2):
                                wv, Bw = wvs[half]
                                bk, Bbk = bank_ring.next()
                                mm_group(bk[:], [(oT[:, kc, s4 * 128:(s4 + 1) * 128], wv[:, kc, :]) for kc in range(8)],
                                         reads=[Bw, BoT], bank_buf=Bbk)
                                xs = X[:, s4, half * 512:(half + 1) * 512]
                                S.op("dve", lambda e, bk=bk, xs=xs: e.tensor_tensor(out=xs, in0=bk[:], in1=xs, op=ALU.add),
                                     reads=[Bbk, BX[s4]], writes=[BX[s4]])
                            norm_front(X[:, s4, :], BX[s4], gb4[:, 2, :], Bgb4, hb2[s4], Bhb2[s4])

                    return [s_load, s_T0conv0] + [(lambda j=j: s_conv(j)) for j in range(1, 4)] + [st_wout, s_q, s_xattn, st_wo]

                def h2_parts(w):
                    xb = w % 2
                    X, BX = xr[xb], Bxr[xb]


                    wds = {}

                    def U(e8):
                        wu, Bwu = wload(s_wup[e8], 8, 512, "wup")
                        hbuf, Bhbuf = hid[e8 % 2], Bhid[e8 % 2]
                        for hc in range(4):
                            bk, Bbk = bank_ring.next()
                            mm_group(bk[:], [(wu[:, kc, hc * 128:(hc + 1) * 128], hTm[:, kc, :]) for kc in range(8)],
                                     reads=[Bwu, BhTm], bank_buf=Bbk)
                            rb = hc % 2
                            S.op("act", lambda e, bk=bk, rb=rb: e.activation(out=rl[rb][:], in_=bk[:], func=AF.Relu), reads=[Bbk], writes=[Brl[rb]])
                            S.op("pool", lambda e, rb=rb, hc=hc: e.tensor_tensor(out=hbuf[:, hc, :], in0=rl[rb][:], in1=rl[rb][:], op=ALU.mult),
                                 reads=[Brl[rb]], writes=[Bhbuf])

                    def Dn(e8):
                        wd, Bwd = wload(s_wdn[e8], 4, 1024, "wdn")
                        hbuf, Bhbuf = hid[e8 % 2], Bhid[e8 % 2]
                        for s4 in range(4):
                            for half in range(2):
                                bk, Bbk = bank_ring.next()
                                mm_group(bk[:], [(hbuf[:, hc, s4 * 128:(s4 + 1) * 128], wd[:, hc, half * 512:(half + 1) * 512]) for hc in range(4)],
                                         reads=[Bwd, Bhbuf], bank_buf=Bbk)
                                xs = X[:, s4, half * 512:(half + 1) * 512]
                                S.op("dve", lambda e, bk=bk, xs=xs: e.tensor_tensor(out=xs, in0=bk[:], in1=xs, op=ALU.add),
                                     reads=[Bbk, BX[s4]], writes=[BX[s4]])

                    def pre_c0():
                        for s4 in range(4):
                            norm_T(hb2[s4], Bhb2[s4], hTm, BhTm, s4 * 128)
                        U(0)
                        U(1)

                    def post():
                        for s4 in range(4):
                            rs, Brs = rms_rstd(X[:, s4, :], BX[s4], D, hb2[s4], Bhb2[s4])
                            S.op("dve", lambda e, s4=s4, rs=rs: e.scalar_tensor_tensor(
                                out=X[:, s4, :], in0=X[:, s4, :], scalar=rs, in1=gb4[:, 3, :], op0=ALU.mult, op1=ALU.mult),
                                reads=[BX[s4], Brs, Bgb4], writes=[BX[s4]])
                        out_ops.append(S.op("act", lambda e: e.dma_start(
                            out=y[w * 512:(w + 1) * 512, :].rearrange("(s p) d -> p s d", p=128), in_=X[:]),
                            reads=BX, dma=True))

                    def mid(k):
                        Dn(k - 1)
                        U(k + 1)

                    def last():
                        Dn(6)
                        Dn(7)

                    return [pre_c0] + [(lambda k=k: mid(k)) for k in range(1, 7)] + [last, post]

                for i in range(NW + 1):
                    A = h1_steps(i) if i < NW else []
                    Bp = h2_parts(i - 1) if i >= 1 else []
                    for k in range(max(len(A), len(Bp))):
                        if k < len(Bp):
                            Bp[k]()
                        if k < len(A):
                            A[k]()
                S.op("sp", lambda e: None, extra_deps=out_ops, nop=True)
                S.op("act", lambda e: None, extra_deps=out_ops, nop=True)
        except _Stop:
            pass
        S.emit(st)
    return nc, S.stats


_CACHE = {}


def kernel(**inputs):
    names = ["x", "mem", "g_mix", "w_in", "conv_w", "g_attn_out", "g_conv_out", "w_out", "g_xattn", "g_mem",
             "w_q_mem", "w_kv_mem", "w_o_mem", "g_mlp", "w_up", "w_down", "g_final"]
    arrs = {k: np.ascontiguousarray(np.asarray(inputs[k], dtype=np.float32)) for k in names}
    if "nc" not in _CACHE:
        _CACHE["nc"] = build_nc()
    nc, stats = _CACHE["nc"]
    n = 8
    in_maps = []
    for b in range(n):
        m = {k: arrs[k] for k in names if k not in ("x", "mem")}
        m["x"] = np.ascontiguousarray(arrs["x"][b])
        m["mem"] = np.ascontiguousarray(arrs["mem"][b])
        in_maps.append(m)
    res = run_bass_kernel_spmd(nc, in_maps, core_ids=list(range(n)))
    return np.stack([np.asarray(r["y"], dtype=np.float32) for r in res.results], axis=0)
```

```python
import numpy as np
from contextlib import ExitStack
import concourse.bass as bass
import concourse.mybir as mybir
from concourse.bass_utils import run_bass_kernel_spmd

F32 = mybir.dt.float32
BF16 = mybir.dt.bfloat16
AF = mybir.ActivationFunctionType
ALU = mybir.AluOpType

S_LEN = 4096
D = 1024
MEM = 256
EPS = 1e-6
NW = 8
NT = 32


class Buf:
    __slots__ = ("w", "r", "name")

    def __init__(self, name=""):
        self.w = None
        self.r = []
        self.name = name


class Op:
    __slots__ = ("eng", "fn", "deps", "has_dep", "sem", "val", "is_dma", "idx")


class Sched:
    ENGS = ("pe", "act", "dve", "pool", "sp")
    N_DMA_SEMS = 40

    def __init__(self, nc):
        self.nc = nc
        self.ops = []
        self.last = {}
        self.dmas = []
        self.stopped = False

    def op(self, eng, fn, reads=(), writes=(), dma=False, extra_deps=(), nop=False, no_barrier=False):
        o = Op()
        if self.stopped:
            return None
        o.eng = eng
        o.fn = fn
        o.is_dma = dma
        o.has_dep = False
        o.sem = None
        o.val = 0
        o.idx = len(self.ops)
        deps = {}
        for b in reads:
            if b.w is not None:
                deps[b.w.idx] = b.w
        for b in writes:
            if b.w is not None:
                deps[b.w.idx] = b.w
            for r in b.r:
                deps[r.idx] = r
        for d in extra_deps:
            if d is not None:
                deps[d.idx] = d
        o.deps = list(deps.values())
        for b in reads:
            b.r.append(o)
        for b in writes:
            b.w = o
            b.r = []
        self.ops.append(o)
        if dma:
            if not no_barrier:
                self.dmas.append(o)
        elif not nop:
            self.last[eng] = o
        return o

    def barrier(self):
        if self.stopped:
            return
        deps = list(self.last.values()) + list(self.dmas)
        self.dmas = []
        for e in self.ENGS:
            self.op(e, lambda eng: None, extra_deps=deps, nop=True)

    def emit(self, stack):
        nc = self.nc

        def skip(d, o):
            return d.eng == "pe" and o.eng == "pe" and not d.is_dma and not o.is_dma

        for o in self.ops:
            for d in o.deps:
                if not skip(d, o):
                    d.has_dep = True
        esem = {e: stack.enter_context(nc.semaphore("sem_" + e)) for e in self.ENGS}
        dsems = [stack.enter_context(nc.semaphore("dsem%d" % i)) for i in range(self.N_DMA_SEMS)]
        dcnt = [0] * self.N_DMA_SEMS
        dlast = [None] * self.N_DMA_SEMS
        cnt = {e: 0 for e in self.ENGS}
        nd = 0
        for o in self.ops:
            if o.is_dma and o.eng == "pool":
                o.sem = stack.enter_context(nc.semaphore("swsem%d" % nd))
                o.val = 16
                nd += 1
            elif o.is_dma:
                i = nd % self.N_DMA_SEMS
                nd += 1
                if dlast[i] is not None:
                    o.deps.append(dlast[i])
                dcnt[i] += 16
                o.sem = dsems[i]
                o.val = dcnt[i]
                dlast[i] = o
            elif o.has_dep:
                cnt[o.eng] += 1
                o.sem = esem[o.eng]
                o.val = cnt[o.eng]
        progs = {e: [] for e in self.ENGS}
        waited = {e: {} for e in self.ENGS}
        nwait = 0
        for o in self.ops:
            w = waited[o.eng]
            waits = {}
            for d in o.deps:
                if skip(d, o):
                    continue
                key = id(d.sem)
                if w.get(key, 0) >= d.val:
                    continue
                w[key] = d.val
                waits[key] = (d.sem, d.val)
            nwait += len(waits)
            progs[o.eng].append((list(waits.values()), o))
        self.stats = dict(n_ops=len(self.ops), n_waits=nwait, counts=dict(cnt), n_dma=nd)

        def replay(eng_name):
            def body(eng):
                for waits, o in progs[eng_name]:
                    for (s, v) in waits:
                        eng.wait_ge(s, v)
                    ins = o.fn(eng)
                    if ins is not None and o.sem is not None:
                        ins.then_inc(o.sem, 16 if o.is_dma else 1)
            return body

        with nc.Block() as block:
            block.tensor(replay("pe"))
            block.scalar(replay("act"))
            block.vector(replay("dve"))
            block.gpsimd(replay("pool"))
            block.sync(replay("sp"))


def ts(start, count, step=1):
    return slice(start, start + (count - 1) * step + 1, step)


class Ring:
    def __init__(self, items):
        self.items = items
        self.i = 0

    def next(self):
        it = self.items[self.i % len(self.items)]
        self.i += 1
        return it


class _Stop(Exception):
    pass


def build_nc(dbg=None):
    dbg = dbg or {}
    nc = bass.Bass("TRN2", target_bir_lowering=False)
    dumps = []

    def din(name, shape):
        return nc.dram_tensor(name, list(shape), F32, kind="ExternalInput").ap()

    x = din("x", (S_LEN, D))
    mem = din("mem", (MEM, D))
    g_mix = din("g_mix", (D,))
    w_in = din("w_in", (D, 3072))
    conv_w = din("conv_w", (3, 512))
    g_attn_out = din("g_attn_out", (512,))
    g_conv_out = din("g_conv_out", (512,))
    w_out = din("w_out", (D, D))
    g_xattn = din("g_xattn", (D,))
    g_mem = din("g_mem", (D,))
    w_q_mem = din("w_q_mem", (D, D))
    w_kv_mem = din("w_kv_mem", (D, 2 * D))
    w_o_mem = din("w_o_mem", (D, D))
    g_mlp = din("g_mlp", (D,))
    w_up = din("w_up", (D, 4 * D))
    w_down = din("w_down", (4 * D, D))
    g_final = din("g_final", (D,))
    y = nc.dram_tensor("y", [S_LEN, D], F32, kind="ExternalOutput").ap()

    def stash(name, shape):
        return nc.dram_tensor(name, list(shape), BF16).ap()

    s_win = stash("s_win", (3, 128, 4096))
    s_wcv = stash("s_wcv", (4, 128, 3072))
    s_wout = stash("s_wout", (2, 128, 4096))
    s_wq = stash("s_wq", (2, 128, 4096))
    s_wkv = stash("s_wkv", (4, 128, 4096))
    s_wo = stash("s_wo", (2, 128, 4096))
    s_wup = stash("s_wup", (8, 128, 4096))
    s_wdn = stash("s_wdn", (8, 128, 4096))

    S = Sched(nc)

    def dump(name, src_ap, shape, dt, reads):
        if S.stopped:
            return
        d = nc.dram_tensor(name, list(shape), dt, kind="ExternalOutput").ap()
        dumps.append(S.op("sp", lambda e: e.dma_start(out=d, in_=src_ap), reads=reads, dma=True))

    def stop():
        S.op("sp", lambda e: None, extra_deps=dumps, nop=True)
        S.stopped = True
    with ExitStack() as st:
        try:
            def sb(name, shape, dt, stack=None):
                return (stack or st).enter_context(nc.sbuf_tensor(name, list(shape), dt))

            pall = [st.enter_context(nc.psum_tensor("pb%d" % i, [128, 512], F32)) for i in range(8)]
            Bpall = [Buf("pb%d" % i) for i in range(8)]
            pall_bf = [p[:].bitcast(BF16) for p in pall]
            pf = pall[2:8]
            Bpf = Bpall[2:8]
            tp_state = {"ring": Ring([(pall_bf[0], Bpall[0]), (pall_bf[1], Bpall[1])])}

            ident = sb("ident", [128, 128], BF16)
            ones_bf = sb("ones_bf", [128, 128], BF16)
            mask4 = sb("mask4", [128, 512], BF16)
            attnT = sb("attnT", [128, 4, S_LEN], BF16)
            ga = sb("ga", [128, 4], F32)
            gc = sb("gc", [128, 4], F32)
            cw = sb("cw", [128, 3, 4], F32)
            rstd_a = sb("rstd_a", [128, NT], F32)
            ssv = sb("ssv", [128, 8], F32)
            rsv = sb("rsv", [128, 8], F32)
            Bconst = Buf("const")
            BattnT = [[Buf("attnT%d_%d" % (p, w)) for w in range(NW)] for p in range(4)]
            Brstd_a = Buf("rstd_a")
            stat_ring = Ring([(i, Buf("ss%d" % i), Buf("rs%d" % i)) for i in range(8)])


            stash_ops = {k: [] for k in ("win0", "win1", "win2", "win", "wout", "wq", "wkv", "wo", "wup", "wdn")}

            def stash_cols(dst_chunk, src, c0, ncol, key):
                kk = src.shape[0] // 128
                S.op("pool", lambda e: e.dma_start(out=dst_chunk.rearrange("p (k c) -> p k c", k=kk),
                                                   in_=src[:, c0:c0 + ncol].rearrange("(k p) c -> p k c", p=128)),
                     writes=[], dma=True, no_barrier=True)
                stash_ops[key].append(S.ops[-1])

            for ci in range(3):
                stash_cols(s_win[ci], w_in, ci * 512, 512, "win%d" % ci)

            def stash_conv():
                for j in range(4):
                    for g in range(3):
                        S.op("pool", lambda e, j=j, g=g: e.dma_start(
                            out=s_wcv[j].rearrange("p (k g c) -> p k g c", k=8, g=3)[:, :, g, :],
                            in_=w_in[:, 1536 + g * 512 + j * 128:1536 + g * 512 + (j + 1) * 128].rearrange("(k p) c -> p k c", p=128)),
                            writes=[], dma=True, no_barrier=True)
                        stash_ops["win"].append(S.ops[-1])

            def stash_rows(dst_chunk, src, r0, nrow, key):
                kk = nrow // 128
                S.op("pool", lambda e: e.dma_start(out=dst_chunk.rearrange("p (k c) -> p k c", k=kk),
                                                   in_=src[r0:r0 + nrow, :].rearrange("(k p) c -> p k c", p=128)),
                     writes=[], dma=True, no_barrier=True)
                stash_ops[key].append(S.ops[-1])

            def mk_consts(e):
                e.memset(ident[:], 0.0)
                e.affine_select(out=ident[:], in_=ident[:], compare_op=ALU.not_equal, fill=1.0,
                                base=0, pattern=[[-1, 128]], channel_multiplier=1)
                e.memset(ones_bf[:], 1.0)
                e.memset(mask4[:], 1.0)
                for blk in range(4):
                    if blk % 2 == 0:
                        e.affine_select(out=mask4[:, blk * 128:(blk + 1) * 128], in_=mask4[:, blk * 128:(blk + 1) * 128],
                                        compare_op=ALU.is_ge, fill=0.0, base=0, pattern=[[-1, 128]], channel_multiplier=1)
                    else:
                        e.affine_select(out=mask4[:, blk * 128:(blk + 1) * 128], in_=mask4[:, blk * 128:(blk + 1) * 128],
                                        compare_op=ALU.is_ge, fill=0.0, base=0, pattern=[[1, 128]], channel_multiplier=-1)
                return None

            S.op("pool", lambda e: e.memset(ident[:], 0.0), writes=[Bconst])
            S.op("pool", lambda e: e.affine_select(out=ident[:], in_=ident[:], compare_op=ALU.not_equal, fill=1.0,
                                                    base=0, pattern=[[-1, 128]], channel_multiplier=1),
                 reads=[Bconst], writes=[Bconst])
            S.op("pool", lambda e: e.memset(ones_bf[:], 1.0), writes=[Bconst])
            S.op("pool", lambda e: e.memset(mask4[:], 1.0), writes=[Bconst])
            for blk in range(4):
                sl = slice(blk * 128, (blk + 1) * 128)
                if blk % 2 == 1:
                    S.op("pool", lambda e, sl=sl: e.affine_select(out=mask4[:, sl], in_=mask4[:, sl], compare_op=ALU.is_ge,
                                                                 fill=0.0, base=0, pattern=[[-1, 128]], channel_multiplier=1),
                         reads=[Bconst], writes=[Bconst])
                else:
                    S.op("pool", lambda e, sl=sl: e.affine_select(out=mask4[:, sl], in_=mask4[:, sl], compare_op=ALU.is_ge,
                                                                 fill=0.0, base=0, pattern=[[1, 128]], channel_multiplier=-1),
                         reads=[Bconst], writes=[Bconst])
            with nc.allow_non_contiguous_dma(reason="tiny gain / conv weight loads"):
                Bgvec = Buf("gvecs")
                Bgcv = Buf("gcv")
                Bcw = [Buf("cw%d" % t) for t in range(3)]
                S.op("act", lambda e: e.dma_start(out=ga[:], in_=g_attn_out.rearrange("(c p) -> p c", p=128), allow_slow_non_contiguous=True),
                     writes=[Bgvec], dma=True)
                S.op("act", lambda e: e.dma_start(out=gc[:], in_=g_conv_out.rearrange("(c p) -> p c", p=128), allow_slow_non_contiguous=True),
                     writes=[Bgcv], dma=True)
                for t in range(3):
                    S.op("act", lambda e, t=t: e.dma_start(out=cw[:, t, :], in_=conv_w[t].rearrange("(c p) -> p c", p=128), allow_slow_non_contiguous=True),
                         writes=[Bcw[t]], dma=True)

            evac_flip = [0]

            def evac(out_ap, in_ap, reads, writes):
                evac_flip[0] ^= 1
                if evac_flip[0]:
                    return S.op("act", lambda e: e.activation(out=out_ap, in_=in_ap, func=AF.Copy), reads=reads, writes=writes)
                return S.op("dve", lambda e: e.tensor_copy(out=out_ap, in_=in_ap), reads=reads, writes=writes)

            def mm_group(out_ap, pairs, reads, bank_buf, first_start=True):
                def fn(e):
                    ins = None
                    n = len(pairs)
                    for i, (l, r) in enumerate(pairs):
                        ins = e.matmul(out_ap, lhsT=l, rhs=r, start=(first_start and i == 0), stop=(i == n - 1),
                                       skip_group_check=True)
                    return ins
                return S.op("pe", fn, reads=reads, writes=[bank_buf])

            def rms_rstd(src_ap, Bsrc, n_feat, scr, Bscr):
                i, Bss, Brs = stat_ring.next()
                S.op("act", lambda e: e.activation(out=scr[:, 0:n_feat], in_=src_ap, func=AF.Square, accum_out=ssv[:, i:i + 1]),
                     reads=[Bsrc], writes=[Bscr, Bss])
                S.op("act", lambda e: e.activation(out=rsv[:, i:i + 1], in_=ssv[:, i:i + 1], func=AF.Ln, scale=1.0 / n_feat, bias=EPS),
                     reads=[Bss], writes=[Brs])
                S.op("act", lambda e: e.activation(out=rsv[:, i:i + 1], in_=rsv[:, i:i + 1], func=AF.Exp, scale=-0.5),
                     reads=[Brs], writes=[Brs])
                return rsv[:, i:i + 1], Brs

            def norm_front(src_ap, Bsrc, gb_ap, Bgb, hb, Bhb):
                rs, Brs = rms_rstd(src_ap, Bsrc, D, hb, Bhb)
                S.op("dve", lambda e: e.scalar_tensor_tensor(out=hb[:], in0=src_ap, scalar=rs, in1=gb_ap,
                                                              op0=ALU.mult, op1=ALU.mult),
                     reads=[Bsrc, Brs, Bgb], writes=[Bhb])

            def norm_to_T(src_ap, Bsrc, gb_ap, Bgb, hb, Bhb, dstT, BdstT, col0):
                norm_front(src_ap, Bsrc, gb_ap, Bgb, hb, Bhb)
                norm_T(hb, Bhb, dstT, BdstT, col0)

            def norm_T(hb, Bhb, dstT, BdstT, col0):
                pt, Bpt = tp_state["ring"].next()

                def tr(e):
                    ins = None
                    for kc in range(8):
                        ins = e.transpose(out=pt[:, kc * 128:(kc + 1) * 128], in_=hb[:, kc * 128:(kc + 1) * 128], identity=ident[:])
                    return ins
                S.op("pe", tr, reads=[Bhb, Bconst], writes=[Bpt])
                evac(dstT[:, :, col0:col0 + 128], pt[:].rearrange("p (k t) -> p k t", k=8), reads=[Bpt], writes=[BdstT])

            with ExitStack() as p1:
                KTA = sb("KTA", [128, 4, S_LEN], BF16, p1)
                VTA = sb("VTA", [128, 4, S_LEN], BF16, p1)
                BK = [[Buf("K%d_%d" % (p, w)) for w in range(NW)] for p in range(4)]
                BV = [[Buf("V%d_%d" % (p, w)) for w in range(NW)] for p in range(4)]
                BQ = BattnT

                with ExitStack() as p1a:
                    gbm = sb("gbm", [128, D], F32, p1a)
                    Bgbm = Buf("gbm")
                    S.op("sp", lambda e: e.dma_start(out=gbm[:], in_=g_mix.partition_broadcast(128)), writes=[Bgbm], dma=True)
                    win_sb = sb("win_sb", [128, 8, 1536], BF16, p1a)
                    Bwin = [Buf("win%d" % i) for i in range(3)]
                    xw = [sb("xw%d" % i, [128, 4, D], F32, p1a) for i in range(2)]
                    Bxw = [[Buf("xw%d_%d" % (i, s4)) for s4 in range(4)] for i in range(2)]
                    hbs = [[sb("hbs%d_%d" % (i, s4), [128, D], BF16, p1a) for s4 in range(4)] for i in range(2)]
                    Bhbs = [[Buf("hbs%d_%d" % (i, s4)) for s4 in range(4)] for i in range(2)]
                    hTw1 = [sb("hTw1_%d" % i, [128, 8, 512], BF16, p1a) for i in range(2)]
                    BhTw1 = [Buf("hTw1_%d" % i) for i in range(2)]
                    proj_ring = Ring(list(zip(pf, Bpf)))

                    def front1(w):
                        b = w % 2
                        S.op("sp", lambda e: e.dma_start(
                            out=xw[b][:], in_=x[w * 512:(w + 1) * 512, :].rearrange("(s p) d -> p s d", p=128)),
                            writes=Bxw[b], dma=True)
                        for s4 in range(4):
                            norm_front(xw[b][:, s4, :], Bxw[b][s4], gbm[:], Bgbm, hbs[b][s4], Bhbs[b][s4])

                    def T1(w):
                        b = w % 2
                        for s4 in range(4):
                            norm_T(hbs[b][s4], Bhbs[b][s4], hTw1[b], BhTw1[b], s4 * 128)

                    def proj1(w):
                        b = w % 2
                        for ci, (dst, Bdst) in enumerate(((attnT, BQ), (KTA, BK), (VTA, BV))):
                            for p in range(4):
                                c0 = ci * 512 + p * 128
                                bk, Bbk = proj_ring.next()
                                mm_group(bk[:], [(win_sb[:, kc, c0:c0 + 128], hTw1[b][:, kc, :]) for kc in range(8)],
                                         reads=[Bwin[ci], BhTw1[b]], bank_buf=Bbk)
                                evac(dst[:, p, w * 512:(w + 1) * 512], bk[:], reads=[Bbk], writes=[Bdst[p][w]])

                    front1(0)
                    front1(1)
                    for ci in range(3):
                        S.op("sp", lambda e, ci=ci: e.dma_start(
                            out=win_sb[:, :, ci * 512:(ci + 1) * 512],
                            in_=s_win[ci].rearrange("p (k c) -> p k c", k=8)),
                            writes=[Bwin[ci]], dma=True, extra_deps=stash_ops["win%d" % ci])
                    T1(0)
                    for w in range(NW):
                        if w + 2 < NW:
                            front1(w + 2)
                        if w + 1 < NW:
                            T1(w + 1)
                        proj1(w)
                    S.barrier()

                with ExitStack() as p1c:
                    Vp = {d: sb("Vp%d" % d, [128, NT, 192], BF16, p1c) for d in (1, 4, 16)}
                    BVp = {d: [Buf("Vp%d_%d" % (d, i)) for i in range(NT)] for d in (1, 4, 16)}
                    QT4 = sb("QT4", [128, NW, 4, 128], BF16, p1c)
                    KT4 = sb("KT4", [128, NW, 4, 128], BF16, p1c)
                    QT16 = sb("QT16", [128, 2, 16, 128], BF16, p1c)
                    KT16 = sb("KT16", [128, 2, 16, 128], BF16, p1c)
                    BQT4, BKT4, BQT16, BKT16 = Buf("QT4"), Buf("KT4"), Buf("QT16"), Buf("KT16")
                    P1 = [sb("P1_%d" % i, [128, 4, 2, 128], BF16, p1c) for i in range(3)]
                    P4 = [sb("P4_%d" % i, [128, 4, 2, 128], BF16, p1c) for i in range(3)]
                    BP1 = [[Buf("P1_%d_%d" % (i, k)) for k in range(2)] for i in range(3)]
                    BP4 = [[Buf("P4_%d_%d" % (i, k)) for k in range(2)] for i in range(3)]
                    P16b = sb("P16b", [128, 16, 128], BF16, p1c)
                    BP16b = [Buf("P16b_%d" % k) for k in range(8)]
                    tmpn = [sb("tmpn%d" % i, [128, 512], F32, p1c) for i in range(2)]
                    Btmpn = [Buf("tmpn%d" % i) for i in range(2)]
                    recn = [sb("recn%d" % i, [128, 512], F32, p1c) for i in range(2)]
                    Brecn = [Buf("recn%d" % i) for i in range(2)]
                    sqp = sb("sqp", [128, S_LEN], BF16, p1c)
                    Bsqp = [Buf("sqp%d" % w) for w in range(NW)]
                    tp_state["ring"] = Ring([(pall_bf[0], Bpall[0]), (pall_bf[7], Bpall[7])])
                    S_ring = Ring([(pall[1], Bpall[1]), (pall[2], Bpall[2]), (pall[3], Bpall[3]), (pall[7], Bpall[7]), (pall[0], Bpall[0])])
                    O_ring = Ring([(pall[4], Bpall[4]), (pall[5], Bpall[5])])
                    ssq_bank, Bssq = pall[6], Bpall[6]

                    for d in (1, 4, 16):
                        S.op("pool", lambda e, d=d: e.memset(Vp[d][:, :, 64:128], 1.0), writes=BVp[d])

                    ssq_started = [False]
                    mflip = [0]

                    def mask_mul(dst, msk, Bd):
                        S.op("dve", lambda e: e.tensor_tensor(out=dst, in0=dst, in1=msk, op=ALU.mult),
                             reads=[Bd, Bconst], writes=[Bd])

                    def do_pair(pair):
                        VTp = VTA[:, pair, :]
                        P16a = VTp.rearrange("p (r a b) -> p r a b", r=16, a=2)
                        BP16a = BV[pair]
                        for d in (1, 4, 16):
                            for t0 in range(0, NT, 8):
                                pt, Bpt = tp_state["ring"].next()

                                def trv(e, d=d, t0=t0, pt=pt):
                                    ins = None
                                    for j in range(8):
                                        t = t0 + j
                                        if d == 1:
                                            src = VTp[:, t * 128:(t + 1) * 128]
                                        elif d == 4:
                                            src = VTp[:, ts(512 * (t // 4) + (t % 4), 128, 4)]
                                        else:
                                            src = VTp[:, ts(2048 * (t // 16) + (t % 16), 128, 16)]
                                        ins = e.transpose(out=pt[:, j * 128:(j + 1) * 128], in_=src, identity=ident[:])
                                    return ins
                                S.op("pe", trv, reads=BV[pair] + [Bconst], writes=[Bpt])
                                evac(Vp[d][:, t0:t0 + 8, :].rearrange("p t (b e) -> p t b e", b=3)[:, :, 0:3:2, :],
                                     pt[:].rearrange("p (t b e) -> p t b e", t=8, b=2),
                                     reads=[Bpt], writes=BVp[d][t0:t0 + 8])
                        Qp, Kp = attnT[:, pair, :], KTA[:, pair, :]
                        S.op("dve", lambda e, Qp=Qp: e.tensor_copy(out=QT16[:], in_=Qp.rearrange("p (n i r) -> p n r i", r=16, i=128)),
                             reads=BQ[pair], writes=[BQT16])
                        S.op("act", lambda e, Kp=Kp: e.activation(out=KT16[:, 0:1, :, :], in_=Kp[:, 0:2048].rearrange("p (n i r) -> p n r i", r=16, i=128), func=AF.Copy),
                             reads=BK[pair], writes=[BKT16])
                        S.op("dve", lambda e, Kp=Kp: e.tensor_copy(out=KT16[:, 1:2, :, :], in_=Kp[:, 2048:4096].rearrange("p (n i r) -> p n r i", r=16, i=128)),
                             reads=BK[pair], writes=[BKT16])
                        S.op("dve", lambda e, Qp=Qp: e.tensor_copy(out=QT4[:], in_=Qp.rearrange("p (n i r) -> p n r i", r=4, i=128)),
                             reads=BQ[pair], writes=[BQT4])
                        S.op("act", lambda e, Kp=Kp: e.activation(out=KT4[:], in_=Kp.rearrange("p (n i r) -> p n r i", r=4, i=128), func=AF.Copy),
                             reads=BK[pair], writes=[BKT4])

                        if dbg.get("stop") == "proj" and pair == 0:
                            dump("d_QT", attnT[:, 0, :], [128, S_LEN], BF16, BQ[0])
                            dump("d_KT", KTA[:, 0, :], [128, S_LEN], BF16, BK[0])
                            dump("d_QT16", QT16[:], [128, 2, 16, 128], BF16, [BQT16])
                            dump("d_KT4", KT4[:], [128, NW, 4, 128], BF16, [BKT4])
                            for d in (1, 4, 16):
                                dump("d_Vp%d" % d, Vp[d][:], [128, NT, 192], BF16, BVp[d])
                            stop()

                        def score_tiles(hh, tiles):
                            for i in range(0, len(tiles), 2):
                                chunk = tiles[i:i + 2]
                                sbank, Bsb = S_ring.next()
                                mms = []
                                rds = []
                                for j, (kT, q, hn, Pd, Bd, rd, _pd) in enumerate(chunk):
                                    n = 256 if hn else 128
                                    mms.append((sbank[:, j * 256:j * 256 + n], kT, q))
                                    rds += rd

                                def fn(e, mms=mms):
                                    ins = None
                                    for k, (o, l, r) in enumerate(mms):
                                        ins = e.matmul(o, lhsT=l, rhs=r, start=(k == 0), stop=(k == len(mms) - 1),
                                                       skip_group_check=True)
                                    return ins
                                S.op("pe", fn, reads=rds, writes=[Bsb])
                                if len(chunk) == 2 and chunk[0][2] and chunk[1][2] and chunk[0][4] is chunk[1][4] and dbg.get("fuse", 1):
                                    Pd0 = chunk[0][3]
                                    Bd = chunk[0][4]
                                    dst = chunk[0][6]
                                    S.op("act", lambda e, dst=dst, sbank=sbank: e.activation(out=dst, in_=sbank[:], func=AF.Exp, scale=0.125),
                                         reads=[Bsb], writes=[Bd])
                                    mask_mul(dst, mask4[:], Bd)
                                else:
                                    for j, (kT, q, hn, Pd, Bd, rd, _pd) in enumerate(chunk[:2]):
                                        n = 256 if hn else 128
                                        src = sbank[:, j * 256:j * 256 + n]
                                        dst = Pd.rearrange("p a b -> p (a b)") if hn else Pd
                                        S.op("act", lambda e, dst=dst, src=src: e.activation(out=dst, in_=src, func=AF.Exp, scale=0.125),
                                             reads=[Bsb], writes=[Bd])
                                        mask_mul(dst, mask4[:, 0:n], Bd)

                        def S16(hh, n2):
                            hp = slice(64 * hh, 64 * hh + 64)
                            tiles = []
                            for r in range(16):
                                kT = KT16[hp, n2, r, :]
                                if n2 == 0:
                                    q = QT16[hp, 0:2, r, :]
                                    Pd = P16a[:, r, :, :]
                                    Bd = BP16a[r // 2]
                                    pairdst = P16a[:, r - 1:r + 1, :, :].rearrange("p r a b -> p (r a b)") if r % 2 == 1 else None
                                    tiles.append([kT, q, True, Pd, Bd, [BKT16, BQT16], pairdst])
                                else:
                                    q = QT16[hp, 1, r, :]
                                    tiles.append([kT, q, False, P16b[:, r, :], BP16b[r // 2], [BKT16, BQT16], None])
                            for i in range(0, 16, 2):
                                tiles[i][6] = tiles[i + 1][6]
                            score_tiles(hh, [tuple(t) for t in tiles])

                        def S1(hh, w):
                            hp = slice(64 * hh, 64 * hh + 64)
                            b = w % 3
                            tiles = []
                            for g in range(4):
                                t = 4 * w + g
                                hn = t + 1 < NT
                                kT = KTA[hp, pair, 128 * t:128 * t + 128]
                                q = attnT[hp, pair, 128 * t:128 * t + (256 if hn else 128)]
                                rd = [BK[pair][w], BQ[pair][w]] + ([BQ[pair][w + 1]] if (g == 3 and hn) else [])
                                Pd = P1[b][:, g, :, :] if hn else P1[b][:, g, 0, :]
                                pairdst = P1[b][:, g - 1:g + 1, :, :].rearrange("p r a b -> p (r a b)") if g % 2 == 1 else None
                                tiles.append((kT, q, hn, Pd, BP1[b][g // 2], rd, pairdst))
                            tiles = [(t[0], t[1], t[2], t[3], t[4], t[5], tiles[(i // 2) * 2 + 1][6]) for i, t in enumerate(tiles)]
                            score_tiles(hh, tiles)

                        def S4(hh, w):
                            hp = slice(64 * hh, 64 * hh + 64)
                            b = w % 3
                            hn = w + 1 < NW
                            tiles = []
                            for r in range(4):
                                kT = KT4[hp, w, r, :]
                                q = QT4[hp, w:w + 2, r, :] if hn else QT4[hp, w, r, :]
                                Pd = P4[b][:, r, :, :] if hn else P4[b][:, r, 0, :]
                                pairdst = P4[b][:, r - 1:r + 1, :, :].rearrange("p r a b -> p (r a b)") if r % 2 == 1 else None
                                tiles.append((kT, q, hn, Pd, BP4[b][r // 2], [BKT4, BQT4], pairdst))
                            tiles = [(t[0], t[1], t[2], t[3], t[4], t[5], tiles[(i // 2) * 2 + 1][6]) for i, t in enumerate(tiles)]
                            score_tiles(hh, tiles)

                        def PV_norm(hh, w):
                            vsl = slice(64 * hh, 64 * hh + 128)
                            nump = slice(64 * hh, 64 * hh + 64)
                            denp = slice(64 * (1 - hh), 64 * (1 - hh) + 64)
                            n2, ww = w // 4, w % 4
                            b, pb = w % 3, (w - 1) % 3
                            ob, Bob = O_ring.next()
                            pv = []
                            for g in range(4):
                                t = 4 * w + g
                                pv.append((ob[:, g * 128:(g + 1) * 128], Vp[1][:, t, vsl], P1[b][:, g, 0, :]))
                                if t >= 1:
                                    prevP = P1[b][:, g - 1, 1, :] if g >= 1 else P1[pb][:, 3, 1, :]
                                    pv.append((ob[:, g * 128:(g + 1) * 128], Vp[1][:, t - 1, vsl], prevP))
                            for r in range(4):
                                pv.append((ob[:, ts(r, 128, 4)], Vp[4][:, 4 * w + r, vsl], P4[b][:, r, 0, :]))
                                if w >= 1:
                                    pv.append((ob[:, ts(r, 128, 4)], Vp[4][:, 4 * (w - 1) + r, vsl], P4[pb][:, r, 1, :]))
                            for r in range(16):
                                csl = slice(32 * ww, 32 * ww + 32)
                                if n2 == 0:
                                    pv.append((ob[:, ts(r, 32, 16)], Vp[16][:, r, vsl], P16a[:, r, 0, csl]))
                                else:
                                    pv.append((ob[:, ts(r, 32, 16)], Vp[16][:, 16 + r, vsl], P16b[:, r, csl]))
                                    pv.append((ob[:, ts(r, 32, 16)], Vp[16][:, r, vsl], P16a[:, r, 1, csl]))

                            def pvfn(e, pv=pv):
                                ins = None
                                for k, (o, l, r) in enumerate(pv):
                                    ins = e.matmul(o, lhsT=l, rhs=r, start=(k == 0), stop=(k == len(pv) - 1),
                                                   skip_group_check=True)
                                return ins
                            S.op("pe", pvfn, reads=BP1[b] + BP1[pb] + BP4[b] + BP4[pb] + BP16a + (BP16b if n2 == 1 else []) + BVp[1] + BVp[4] + BVp[16],
                                 writes=[Bob])
                            nb = (hh * NW + w) % 2
                            S.op("act", lambda e: e.activation(out=recn[nb][denp, :], in_=ob[denp, :], func=AF.Ln),
                                 reads=[Bob], writes=[Brecn[nb]])
                            S.op("act", lambda e: e.activation(out=recn[nb][denp, :], in_=recn[nb][denp, :], func=AF.Exp, scale=-1.0),
                                 reads=[Brecn[nb]], writes=[Brecn[nb]])
                            S.op("dve", lambda e: e.tensor_tensor(out=tmpn[nb][nump, :], in0=ob[nump, :], in1=recn[nb][denp, :], op=ALU.mult),
                                 reads=[Bob, Brecn[nb]], writes=[Btmpn[nb]])
                            S.op("pool", lambda e: e.tensor_tensor(out=sqp[nump, w * 512:(w + 1) * 512], in0=tmpn[nb][nump, :],
                                                                   in1=tmpn[nb][nump, :], op=ALU.mult),
                                 reads=[Btmpn[nb]], writes=[Bsqp[w]])
                            S.op("pool", lambda e: e.tensor_scalar(
                                out=attnT[nump, pair, w * 512:(w + 1) * 512], in0=tmpn[nb][nump, :],
                                scalar1=ga[nump, pair:pair + 1], scalar2=1.0, op0=ALU.mult, op1=ALU.mult),
                                reads=[Btmpn[nb], Bgvec], writes=[BattnT[pair][w]])

                        for hh in range(2):
                            S16(hh, 0)
                            S1(hh, 0)
                            S4(hh, 0)
                            for w in range(NW):
                                if w + 1 < NW:
                                    if w + 1 == 4:
                                        S16(hh, 1)
                                    S1(hh, w + 1)
                                    S4(hh, w + 1)
                                PV_norm(hh, w)
                        for w in range(NW):
                            def ssfn(e, w=w, first=not ssq_started[0]):
                                ins = None
                                for s4 in range(4):
                                    tt = 4 * w + s4
                                    ins = e.matmul(ssq_bank[:, tt:tt + 1], lhsT=sqp[:, tt * 128:(tt + 1) * 128], rhs=ones_bf[:, 0:1],
                                                   start=(first and s4 == 0), stop=True, skip_group_check=True)
                                return ins
                            S.op("pe", ssfn, reads=[Bsqp[w], Bconst], writes=[Bssq])
                            ssq_started[0] = True
                        if dbg.get("stop") == "attn0" and pair == 0:
                            dump("d_attnT", attnT[:, 0, :], [128, S_LEN], BF16, BattnT[0])
                            stop()
                        if not dbg.get("stash2", True):
                            pass
                        elif pair == 0:
                            stash_conv()
                            for c in range(4):
                                stash_cols(s_wkv[c], w_kv_mem, c * 512, 512, "wkv")
                            for c in range(2):
                                stash_cols(s_wout[c], w_out, c * 512, 512, "wout")
                            for c in range(2):
                                stash_cols(s_wq[c], w_q_mem, c * 512, 512, "wq")
                            for c in range(2):
                                stash_cols(s_wo[c], w_o_mem, c * 512, 512, "wo")
                        elif pair == 1:
                            for c in range(8):
                                stash_cols(s_wup[c], w_up, c * 512, 512, "wup")
                        elif pair == 2:
                            for c in range(8):
                                stash_rows(s_wdn[c], w_down, c * 512, 512, "wdn")

                    for pair_i in range(dbg.get("pairs", 4)):
                        do_pair(pair_i)

                    S.op("act", lambda e: e.activation(out=rstd_a[:], in_=ssq_bank[:, 0:NT], func=AF.Ln, scale=1.0 / 512, bias=EPS),
                         reads=[Bssq], writes=[Brstd_a])
                    S.op("act", lambda e: e.activation(out=rstd_a[:], in_=rstd_a[:], func=AF.Exp, scale=-0.5),
                         reads=[Brstd_a], writes=[Brstd_a])
                    S.barrier()
                    tp_state["ring"] = Ring([(pall_bf[0], Bpall[0]), (pall_bf[1], Bpall[1])])
                    if dbg.get("stop") == "1c":
                        dump("d_attnT", attnT[:], [128, 4, S_LEN], BF16, sum(BattnT, []))
                        dump("d_rstd_a", rstd_a[:], [128, NT], F32, [Brstd_a])
                        stop()

            with ExitStack() as p2:
                gb4 = sb("gb4", [128, 4, D], F32, p2)
                Bgb4 = Buf("gb4")
                for i, gvec in enumerate((g_mix, g_xattn, g_mlp, g_final)):
                    S.op("sp", lambda e, i=i, gvec=gvec: e.dma_start(out=gb4[:, i, :], in_=gvec.partition_broadcast(128)),
                         writes=[Bgb4], dma=True)
                memKT = sb("memKT", [128, 8, MEM], BF16, p2)
                memV = sb("memV", [128, 2, D], BF16, p2)
                BmemKT, BmemV = Buf("memKT"), Buf("memV")
                uw = sb("uw", [128, 514], F32, p2)
                Buw = Buf("uw")
                ucar = sb("ucar", [128, 4, 2], F32, p2)
                Bucar = [Buf("ucar%d" % j) for j in range(4)]
                hTm = sb("hTm", [128, 8, 512], BF16, p2)
                BhTm = Buf("hTm")
                xr = [sb("xr%d" % i, [128, 4, D], F32, p2) for i in range(2)]
                Bxr = [[Buf("xr%d_%d" % (i, s4)) for s4 in range(4)] for i in range(2)]
                hTw = sb("hTw", [128, 8, 512], BF16, p2)
                BhTw = Buf("hTw")
                hb2 = [sb("hb2_%d" % i, [128, D], BF16, p2) for i in range(4)]
                Bhb2 = [Buf("hb2_%d" % i) for i in range(4)]
                xcs = sb("xcs", [128, 512], F32, p2)
                acc = sb("acc", [128, 512], F32, p2)
                ycv = sb("ycv", [128, 512], F32, p2)
                Bxcs, Bacc, Bycv = Buf("xcs"), Buf("acc"), Buf("ycv")
                sqc = sb("sqc", [128, 4, 512], BF16, p2)
                convT = sb("convT", [128, 4, 512], BF16, p2)
                Bsqc, BconvT = Buf("sqc"), Buf("convT")
                rstd_c = sb("rstd_c", [128, 4], F32, p2)
                Brstd_c = Buf("rstd_c")
                NRING = 6
                ring_t = [sb("wring%d" % i, [128, 4096], BF16, p2) for i in range(NRING)]
                wring = Ring([(ring_t[i], Buf("wring%d" % i)) for i in range(NRING)])
                qT = sb("qT", [128, 8, 512], BF16, p2)
                BqT = Buf("qT")
                PX = [sb("PX%d" % i, [128, 2, 512], BF16, p2) for i in range(2)]
                BPX = [Buf("PX%d" % i) for i in range(2)]
                recx = [sb("recx%d" % i, [128, 512], F32, p2) for i in range(1)] * 2
                Brecx = [Buf("recx%d" % i) for i in range(1)] * 2
                oT = hTw
                BoT = BhTw
                rl = [sb("rl%d" % i, [128, 512], F32, p2) for i in range(2)]
                Brl = [Buf("rl%d" % i) for i in range(2)]
                hid = [sb("hid%d" % i, [128, 4, 512], BF16, p2) for i in range(2)]
                Bhid = [Buf("hid%d" % i) for i in range(2)]
                bank_ring = Ring(list(zip(pf, Bpf)))

                def wload(src_ap, a, b, dep_key):
                    t, Bt = wring.next()
                    view = t[:, 0:a * b].rearrange("p (a b) -> p a b", a=a)
                    S.op("sp", lambda e: e.dma_start(out=view, in_=src_ap.rearrange("p (a b) -> p a b", a=a)), writes=[Bt], dma=True,
                         extra_deps=stash_ops[dep_key])
                    return view, Bt

                S.op("pool", lambda e: e.memset(ucar[:], 0.0), writes=Bucar)

                with ExitStack() as pm:
                    mt_ = [xr[0][:, 0, :], xr[0][:, 1, :]]
                    Bmt = [Bxr[0][0], Bxr[0][1]]
                    gbmem = xr[0][:, 2, :]
                    Bgbmem = Bxr[0][2]
                    memhT = hTw[:, :, 0:MEM]
                    BmemhT = BhTw
                    S.op("sp", lambda e: e.dma_start(out=gbmem, in_=g_mem.partition_broadcast(128)), writes=[Bgbmem], dma=True)
                    for i in range(2):
                        S.op("sp", lambda e, i=i: e.dma_start(out=mt_[i], in_=mem[i * 128:(i + 1) * 128, :]), writes=[Bmt[i]], dma=True)
                        norm_to_T(mt_[i], Bmt[i], gbmem, Bgbmem, hb2[i], Bhb2[i], memhT, BmemhT, i * 128)
                    for half in range(2):
                        wv, Bw = wload(s_wkv[half], 8, 512, "wkv")
                        for cc in range(4):
                            c = half * 4 + cc
                            bk, Bbk = bank_ring.next()
                            mm_group(bk[:, 0:MEM], [(wv[:, kc, cc * 128:(cc + 1) * 128], memhT[:, kc, :]) for kc in range(8)],
                                     reads=[Bw, BmemhT], bank_buf=Bbk)
                            evac(memKT[:, c, :], bk[:, 0:MEM], reads=[Bbk], writes=[BmemKT])
                    for half in range(2):
                        wv, Bw = wload(s_wkv[2 + half], 8, 512, "wkv")
                        for mt in range(2):
                            bk, Bbk = bank_ring.next()
                            mm_group(bk[:], [(memhT[:, kc, mt * 128:(mt + 1) * 128], wv[:, kc, :]) for kc in range(8)],
                                     reads=[Bw, BmemhT], bank_buf=Bbk)
                            evac(memV[:, mt, half * 512:(half + 1) * 512], bk[:], reads=[Bbk], writes=[BmemV])

                out_ops = []

                def h1_steps(w):
                    xb = w % 2
                    X, BX = xr[xb], Bxr[xb]

                    def s_load():
                        S.op("sp", lambda e: e.dma_start(
                            out=X[:], in_=x[w * 512:(w + 1) * 512, :].rearrange("(s p) d -> p s d", p=128)),
                            writes=BX, dma=True)
                        for s4 in range(4):
                            norm_front(X[:, s4, :], BX[s4], gb4[:, 0, :], Bgb4, hb2[s4], Bhb2[s4])

                    def s_T0conv0():
                        for s4 in range(4):
                            norm_T(hb2[s4], Bhb2[s4], hTw, BhTw, s4 * 128)
                        s_conv(0)

                    def s_conv(j):
                        t, Bt = wring.next()
                        wv = t[:, 0:8 * 384].rearrange("p (a b) -> p a b", a=8)
                        S.op("sp", lambda e: e.dma_start(out=wv, in_=s_wcv[j].rearrange("p (a b) -> p a b", a=8)),
                             writes=[Bt], dma=True, extra_deps=stash_ops["win"])
                        banks = [bank_ring.next() for _ in range(3)]
                        for ci in range(3):
                            mm_group(banks[ci][0][:], [(wv[:, kc, ci * 128:(ci + 1) * 128], hTw[:, kc, :]) for kc in range(8)],
                                     reads=[Bt, BhTw], bank_buf=banks[ci][1])
                        (bgp, Bbg), (cgp, Bcg), (xcp, Bxc) = banks
                        S.op("pool", lambda e: e.tensor_copy(out=uw[:, 0:2], in_=ucar[:, j, :]), reads=[Bucar[j]], writes=[Buw])
                        S.op("act", lambda e: e.activation(out=xcs[:], in_=xcp[:], func=AF.Copy), reads=[Bxc], writes=[Bxcs])
                        S.op("act", lambda e: e.activation(out=uw[:, 2:514], in_=cgp[:], func=AF.Copy), reads=[Bcg], writes=[Buw])
                        S.op("act", lambda e: e.activation(out=ycv[:], in_=bgp[:], func=AF.Copy), reads=[Bbg], writes=[Bycv])
                        S.op("pool", lambda e: e.tensor_tensor(out=uw[:, 2:514], in0=uw[:, 2:514], in1=xcs[:], op=ALU.mult),
                             reads=[Buw, Bxcs], writes=[Buw])
                        S.op("pool", lambda e: e.tensor_scalar(out=acc[:], in0=uw[:, 2:514], scalar1=cw[:, 2, j:j + 1], scalar2=1.0,
                                                               op0=ALU.mult, op1=ALU.mult),
                             reads=[Buw] + Bcw, writes=[Bacc])
                        for tap, sl in ((1, slice(1, 513)), (0, slice(0, 512))):
                            S.op("pool", lambda e, tap=tap, sl=sl: e.tensor_scalar(out=xcs[:], in0=uw[:, sl], scalar1=cw[:, tap, j:j + 1], scalar2=1.0,
                                                                                   op0=ALU.mult, op1=ALU.mult),
                                 reads=[Buw] + Bcw, writes=[Bxcs])
                            S.op("pool", lambda e: e.tensor_tensor(out=acc[:], in0=acc[:], in1=xcs[:], op=ALU.add),
                                 reads=[Bacc, Bxcs], writes=[Bacc])
                        S.op("pool", lambda e: e.tensor_tensor(out=ycv[:], in0=ycv[:], in1=acc[:], op=ALU.mult),
                             reads=[Bycv, Bacc], writes=[Bycv])
                        S.op("act", lambda e: e.activation(out=sqc[:, j, :], in_=ycv[:], func=AF.Square), reads=[Bycv], writes=[Bsqc])
                        S.op("pool", lambda e: e.tensor_scalar(out=convT[:, j, :], in0=ycv[:], scalar1=gc[:, j:j + 1], scalar2=1.0,
                                                               op0=ALU.mult, op1=ALU.mult),
                             reads=[Bycv, Bgcv], writes=[BconvT])
                        S.op("pool", lambda e: e.tensor_copy(out=ucar[:, j, :], in_=uw[:, 512:514]), reads=[Buw], writes=[Bucar[j]])

                    def st_wout():
                        bk, Bbk = bank_ring.next()

                        def ssc(e):
                            ins = None
                            k = 0
                            for s4 in range(4):
                                for j in range(4):
                                    ins = e.matmul(bk[:, s4:s4 + 1], lhsT=sqc[:, j, s4 * 128:(s4 + 1) * 128], rhs=ones_bf[:, 0:1],
                                                   start=(k == 0), stop=(k == 15), skip_group_check=True)
                                    k += 1
                            return ins
                        S.op("pe", ssc, reads=[Bsqc, Bconst], writes=[Bbk])
                        S.op("act", lambda e: e.activation(out=rstd_c[:], in_=bk[:, 0:4], func=AF.Ln, scale=1.0 / 512, bias=EPS),
                             reads=[Bbk], writes=[Brstd_c])
                        S.op("act", lambda e: e.activation(out=rstd_c[:], in_=rstd_c[:], func=AF.Exp, scale=-0.5),
                             reads=[Brstd_c], writes=[Brstd_c])
                        wvs = [wload(s_wout[half], 8, 512, "wout")
                               for half in range(2)]
                        for s4 in range(4):
                            for half in range(2):
                                wv, Bw = wvs[half]
                                tsl = slice(w * 512 + s4 * 128, w * 512 + (s4 + 1) * 128)
                                pa, Bpa = bank_ring.next()
                                pc, Bpc = bank_ring.next()
                                mm_group(pa[:], [(attnT[:, kc, tsl], wv[:, kc, :]) for kc in range(4)],
                                         reads=[Bw] + [BattnT[p][w] for p in range(4)], bank_buf=Bpa)
                                mm_group(pc[:], [(convT[:, kc, s4 * 128:(s4 + 1) * 128], wv[:, 4 + kc, :]) for kc in range(4)],
                                         reads=[Bw, BconvT], bank_buf=Bpc)
                                xs = X[:, s4, half * 512:(half + 1) * 512]
                                tt = 4 * w + s4
                                S.op("dve", lambda e, pa=pa, xs=xs, tt=tt: e.scalar_tensor_tensor(
                                    out=xs, in0=pa[:], scalar=rstd_a[:, tt:tt + 1], in1=xs, op0=ALU.mult, op1=ALU.add),
                                    reads=[Bpa, Brstd_a, BX[s4]], writes=[BX[s4]])
                                S.op("dve", lambda e, pc=pc, xs=xs, s4=s4: e.scalar_tensor_tensor(
                                    out=xs, in0=pc[:], scalar=rstd_c[:, s4:s4 + 1], in1=xs, op0=ALU.mult, op1=ALU.add),
                                    reads=[Bpc, Brstd_c, BX[s4]], writes=[BX[s4]])
                            norm_front(X[:, s4, :], BX[s4], gb4[:, 1, :], Bgb4, hb2[s4], Bhb2[s4])

                    def s_q():
                        for s4 in range(4):
                            norm_T(hb2[s4], Bhb2[s4], hTw, BhTw, s4 * 128)
                        for half in range(2):
                            wv, Bw = wload(s_wq[half], 8, 512, "wq")
                            for cc in range(4):
                                c = half * 4 + cc
                                bk, Bbk = bank_ring.next()
                                mm_group(bk[:], [(wv[:, kc, cc * 128:(cc + 1) * 128], hTw[:, kc, :]) for kc in range(8)],
                                         reads=[Bw, BhTw], bank_buf=Bbk)
                                evac(qT[:, c, :], bk[:], reads=[Bbk], writes=[BqT])

                    def s_xattn():
                        for h in range(4):
                            pb = h % 2
                            for mt in range(2):
                                bk, Bbk = bank_ring.next()
                                mm_group(bk[:], [(memKT[:, 2 * h + cc, mt * 128:(mt + 1) * 128], qT[:, 2 * h + cc, :]) for cc in range(2)],
                                         reads=[BmemKT, BqT], bank_buf=Bbk)
                                S.op("act", lambda e, bk=bk, pb=pb, mt=mt: e.activation(out=PX[pb][:, mt, :], in_=bk[:], func=AF.Exp, scale=1.0 / 16),
                                     reads=[Bbk], writes=[BPX[pb]])
                            bk, Bbk = bank_ring.next()
                            mm_group(bk[:], [(ones_bf[:], PX[pb][:, mt, :]) for mt in range(2)], reads=[Bconst, BPX[pb]], bank_buf=Bbk)
                            S.op("act", lambda e, bk=bk, pb=pb: e.activation(out=recx[pb][:], in_=bk[:], func=AF.Ln), reads=[Bbk], writes=[Brecx[pb]])
                            S.op("act", lambda e, pb=pb: e.activation(out=recx[pb][:], in_=recx[pb][:], func=AF.Exp, scale=-1.0),
                                 reads=[Brecx[pb]], writes=[Brecx[pb]])
                            for cc in range(2):
                                c = 2 * h + cc
                                bk, Bbk = bank_ring.next()
                                mm_group(bk[:], [(memV[:, mt, c * 128:(c + 1) * 128], PX[pb][:, mt, :]) for mt in range(2)],
                                         reads=[BmemV, BPX[pb]], bank_buf=Bbk)
                                S.op("dve", lambda e, bk=bk, pb=pb, c=c: e.tensor_tensor(out=oT[:, c, :], in0=bk[:], in1=recx[pb][:], op=ALU.mult),
                                     reads=[Bbk, Brecx[pb]], writes=[BoT])

                    def st_wo():
                        wvs = [wload(s_wo[half], 8, 512, "wo")
                               for half in range(2)]
                        for s4 in range(4):
                            for half in range(2):
                                wv, Bw = wvs[half]
                                bk, Bbk = bank_ring.next()
                                mm_group(bk[:], [(oT[:, kc, s4 * 128:(s4 + 1) * 128], wv[:, kc, :]) for kc in range(8)],
                                         reads=[Bw, BoT], bank_buf=Bbk)
                                xs = X[:, s4, half * 512:(half + 1) * 512]
                                S.op("dve", lambda e, bk=bk, xs=xs: e.tensor_tensor(out=xs, in0=bk[:], in1=xs, op=ALU.add),
                                     reads=[Bbk, BX[s4]], writes=[BX[s4]])
                            norm_front(X[:, s4, :], BX[s4], gb4[:, 2, :], Bgb4, hb2[s4], Bhb2[s4])

                    return [s_load, s_T0conv0] + [(lambda j=j: s_conv(j)) for j in range(1, 4)] + [st_wout, s_q, s_xattn, st_wo]

                def h2_parts(w):
                    xb = w % 2
                    X, BX = xr[xb], Bxr[xb]


                    wds = {}

                    def U(e8):
                        wu, Bwu = wload(s_wup[e8], 8, 512, "wup")
                        hbuf, Bhbuf = hid[e8 % 2], Bhid[e8 % 2]
                        for hc in range(4):
                            bk, Bbk = bank_ring.next()
                            mm_group(bk[:], [(wu[:, kc, hc * 128:(hc + 1) * 128], hTm[:, kc, :]) for kc in range(8)],
                                     reads=[Bwu, BhTm], bank_buf=Bbk)
                            rb = hc % 2
                            S.op("act", lambda e, bk=bk, rb=rb: e.activation(out=rl[rb][:], in_=bk[:], func=AF.Relu), reads=[Bbk], writes=[Brl[rb]])
                            S.op("pool", lambda e, rb=rb, hc=hc: e.tensor_tensor(out=hbuf[:, hc, :], in0=rl[rb][:], in1=rl[rb][:], op=ALU.mult),
                                 reads=[Brl[rb]], writes=[Bhbuf])

                    def Dn(e8):
                        wd, Bwd = wload(s_wdn[e8], 4, 1024, "wdn")
                        hbuf, Bhbuf = hid[e8 % 2], Bhid[e8 % 2]
                        for s4 in range(4):
                            for half in range(2):
                                bk, Bbk = bank_ring.next()
                                mm_group(bk[:], [(hbuf[:, hc, s4 * 128:(s4 + 1) * 128], wd[:, hc, half * 512:(half + 1) * 512]) for hc in range(4)],
                                         reads=[Bwd, Bhbuf], bank_buf=Bbk)
                                xs = X[:, s4, half * 512:(half + 1) * 512]
                                S.op("dve", lambda e, bk=bk, xs=xs: e.tensor_tensor(out=xs, in0=bk[:], in1=xs, op=ALU.add),
                                     reads=[Bbk, BX[s4]], writes=[BX[s4]])

                    def pre_c0():
                        for s4 in range(4):
                            norm_T(hb2[s4], Bhb2[s4], hTm, BhTm, s4 * 128)
                        U(0)
                        U(1)

                    def post():
                        for s4 in range(4):
                            rs, Brs = rms_rstd(X[:, s4, :], BX[s4], D, hb2[s4], Bhb2[s4])
                            S.op("dve", lambda e, s4=s4, rs=rs: e.scalar_tensor_tensor(
                                out=X[:, s4, :], in0=X[:, s4, :], scalar=rs, in1=gb4[:, 3, :], op0=ALU.mult, op1=ALU.mult),
                                reads=[BX[s4], Brs, Bgb4], writes=[BX[s4]])
                        out_ops.append(S.op("act", lambda e: e.dma_start(
                            out=y[w * 512:(w + 1) * 512, :].rearrange("(s p) d -> p s d", p=128), in_=X[:]),
                            reads=BX, dma=True))

                    def mid(k):
                        Dn(k - 1)
                        U(k + 1)

                    def last():
                        Dn(6)
                        Dn(7)

                    return [pre_c0] + [(lambda k=k: mid(k)) for k in range(1, 7)] + [last, post]

                for i in range(NW + 1):
                    A = h1_steps(i) if i < NW else []
                    Bp = h2_parts(i - 1) if i >= 1 else []
                    for k in range(max(len(A), len(Bp))):
                        if k < len(Bp):
                            Bp[k]()
                        if k < len(A):
                            A[k]()
                S.op("sp", lambda e: None, extra_deps=out_ops, nop=True)
                S.op("act", lambda e: None, extra_deps=out_ops, nop=True)
        except _Stop:
            pass
        S.emit(st)
    return nc, S.stats


_CACHE = {}


def kernel(**inputs):
    names = ["x", "mem", "g_mix", "w_in", "conv_w", "g_attn_out", "g_conv_out", "w_out", "g_xattn", "g_mem",
             "w_q_mem", "w_kv_mem", "w_o_mem", "g_mlp", "w_up", "w_down", "g_final"]
    arrs = {k: np.ascontiguousarray(np.asarray(inputs[k], dtype=np.float32)) for k in names}
    if "nc" not in _CACHE:
        _CACHE["nc"] = build_nc()
    nc, stats = _CACHE["nc"]
    n = 8
    in_maps = []
    for b in range(n):
        m = {k: arrs[k] for k in names if k not in ("x", "mem")}
        m["x"] = np.ascontiguousarray(arrs["x"][b])
        m["mem"] = np.ascontiguousarray(arrs["mem"][b])
        in_maps.append(m)
    res = run_bass_kernel_spmd(nc, in_maps, core_ids=list(range(n)))
    return np.stack([np.asarray(r["y"], dtype=np.float32) for r in res.results], axis=0)
```
